# Optimizing a Trainium2 kernel written in Bass

```python
import jax, jax.numpy as jnp
from jax import lax
import numpy as np

D_MODEL = 2048
BATCH = 4
SEQ = 2048
DEPTH = 2

CHUNK = 64
N_META = 16
FRONT_PAD = CHUNK - N_META
Q_BLOCK = 128
EPS = 1e-6

RET_WIDTH = D_MODEL // 2
RET_DV = 128
RET_HEADS = RET_WIDTH // RET_DV
RET_DK = 128
RET_QK = RET_HEADS * RET_DK

DIFF_WIDTH = D_MODEL // 2
DIFF_DH = 64
DIFF_DV = 2 * DIFF_DH
DIFF_HEADS = DIFF_WIDTH // DIFF_DV
DIFF_QK = DIFF_HEADS * 2 * DIFF_DH

MIX_WIDTH = RET_WIDTH + DIFF_WIDTH
IN_SIZES = [RET_QK, RET_QK, RET_WIDTH, RET_WIDTH, DIFF_QK, DIFF_QK, DIFF_WIDTH, DIFF_WIDTH]
IN_WIDTH = sum(IN_SIZES)

kernel_name = 'hymba_retention_diffattn_chunk_causal'


def rms_norm(x, g):
    xf = x.astype(jnp.float32)
    y = xf * lax.rsqrt(jnp.mean(xf * xf, axis=-1, keepdims=True) + EPS)
    return (y * g).astype(x.dtype)


def head_layer_norm(o, g):
    B, L, H, E = o.shape
    of = o.astype(jnp.float32)
    mu = jnp.mean(of, axis=-1, keepdims=True)
    var = jnp.mean((of - mu) ** 2, axis=-1, keepdims=True)
    y = ((of - mu) * lax.rsqrt(var + EPS)).reshape(B, L, H * E)
    return (y * g).astype(o.dtype)


def head_rms_norm(o, g):
    B, L, H, E = o.shape
    of = o.astype(jnp.float32)
    y = (of * lax.rsqrt(jnp.mean(of * of, axis=-1, keepdims=True) + EPS)).reshape(B, L, H * E)
    return (y * g).astype(o.dtype)


def retention(q, k, v, valid):
    B, L, H, DK = q.shape
    DV = v.shape[-1]
    N = L // CHUNK
    log_g = jnp.log(1.0 - 2.0 ** (-5.0 - jnp.arange(H, dtype=jnp.float32)))
    k = k * (valid[None, :, None, None].astype(k.dtype) * (DK ** -0.5))
    qc = q.reshape(B, N, CHUNK, H, DK)
    kc = k.reshape(B, N, CHUNK, H, DK)
    vc = v.reshape(B, N, CHUNK, H, DV)
    idx = jnp.arange(CHUNK, dtype=jnp.float32)
    dist = jnp.abs(idx[:, None] - idx[None, :])
    dmat = jnp.exp(log_g[:, None, None] * dist).astype(v.dtype)
    s = jnp.einsum('bnthd,bnshd->bnhts', qc, kc) * dmat
    intra = jnp.einsum('bnhts,bnshe->bnthe', s, vc)
    zeta = jnp.exp(log_g[:, None] * (CHUNK - 1 - idx)[None, :]).astype(v.dtype)
    kv = jnp.einsum('bnshd,hs,bnshe->bnhde', kc, zeta, vc)
    chunk_decay = jnp.exp(log_g * CHUNK).astype(v.dtype)[None, :, None, None]

    def step(state, kv_n):
        return chunk_decay * state + kv_n, state

    init = jnp.zeros((B, H, DK, DV), kv.dtype)
    _, prev = lax.scan(step, init, jnp.moveaxis(kv, 1, 0))
    prev = jnp.moveaxis(prev, 0, 1)
    xi = jnp.exp(log_g[:, None] * (idx + 1.0)[None, :]).astype(v.dtype)
    cross = jnp.einsum('bnthd,bnhde,ht->bnthe', qc, prev, xi)
    return (intra + cross).reshape(B, L, H, DV)


def diff_attention(q, k, v, valid, lam):
    B, L, H, _, DH = q.shape
    nb = L // Q_BLOCK
    slopes = 2.0 ** (-8.0 * jnp.arange(1, H + 1, dtype=jnp.float32) / H)
    kpos = jnp.arange(L)
    kchunk = kpos // CHUNK
    qb_all = jnp.moveaxis(q.reshape(B, nb, Q_BLOCK, H, 2, DH), 1, 0)
    starts = jnp.arange(nb, dtype=jnp.int32) * Q_BLOCK
    scale = DH ** -0.5

    def block(args):
        qb, start = args
        qpos = start + jnp.arange(Q_BLOCK, dtype=jnp.int32)
        dist = jnp.abs(qpos[:, None] - kpos[None, :]).astype(jnp.float32)
        bias = -slopes[:, None, None] * dist
        allowed = (kchunk[None, :] <= (qpos // CHUNK)[:, None]) & valid[None, :]
        s = jnp.einsum('bqhcd,bkhcd->bhcqk', qb, k).astype(jnp.float32) * scale
        s = jnp.where(allowed[None, None, None], s + bias[None, :, None], -jnp.inf)
        p = jax.nn.softmax(s, axis=-1)
        a = p[:, :, 0] - lam * p[:, :, 1]
        return jnp.einsum('bhqk,bkhe->bqhe', a.astype(v.dtype), v)

    o = lax.map(block, (qb_all, starts))
    return jnp.moveaxis(o, 0, 1).reshape(B, L, H, v.shape[-1])


def hybrid_layer(h, valid, g_norm, w_in, w_out, g_ret, g_diff, lq1, lk1, lq2, lk2, lam_init):
    B, L, _ = h.shape
    u = rms_norm(h, g_norm)
    proj = u @ w_in
    offs = np.cumsum(IN_SIZES)[:-1].tolist()
    rq, rk, rv, rgate, dq, dk, dv, dgate = jnp.split(proj, offs, axis=-1)
    r = retention(rq.reshape(B, L, RET_HEADS, RET_DK), rk.reshape(B, L, RET_HEADS, RET_DK),
                  rv.reshape(B, L, RET_HEADS, RET_DV), valid)
    r = head_layer_norm(r, g_ret) * jax.nn.silu(rgate)
    lam = (jnp.exp(jnp.sum(lq1.astype(jnp.float32) * lk1.astype(jnp.float32)))
           - jnp.exp(jnp.sum(lq2.astype(jnp.float32) * lk2.astype(jnp.float32))) + lam_init)
    d = diff_attention(dq.reshape(B, L, DIFF_HEADS, 2, DIFF_DH), dk.reshape(B, L, DIFF_HEADS, 2, DIFF_DH),
                       dv.reshape(B, L, DIFF_HEADS, DIFF_DV), valid, lam)
    d = head_rms_norm(d, g_diff) * (1.0 - lam_init) * jax.nn.silu(dgate)
    y = jnp.concatenate([r, d], axis=-1) @ w_out
    return h + y


def setup_inputs(seed: int = 0) -> dict:
    key = jax.random.key(seed)
    ks = jax.random.split(key, 13)
    f32 = jnp.float32
    return {
        'x': jax.random.normal(ks[0], (BATCH, SEQ, D_MODEL), f32),
        'meta_tokens': jax.random.normal(ks[1], (N_META, D_MODEL), f32),
        'norm_g': 1.0 + 0.01 * jax.random.normal(ks[2], (DEPTH, D_MODEL), f32),
        'w_in': jax.random.normal(ks[3], (DEPTH, D_MODEL, IN_WIDTH), f32) * D_MODEL ** -0.5,
        'w_out': jax.random.normal(ks[4], (DEPTH, MIX_WIDTH, D_MODEL), f32) * MIX_WIDTH ** -0.5,
        'ret_norm_g': 1.0 + 0.01 * jax.random.normal(ks[5], (DEPTH, RET_WIDTH), f32),
        'diff_norm_g': 1.0 + 0.01 * jax.random.normal(ks[6], (DEPTH, DIFF_WIDTH), f32),
        'lambda_q1': 0.1 * jax.random.normal(ks[7], (DEPTH, DIFF_DH), f32),
        'lambda_k1': 0.1 * jax.random.normal(ks[8], (DEPTH, DIFF_DH), f32),
        'lambda_q2': 0.1 * jax.random.normal(ks[9], (DEPTH, DIFF_DH), f32),
        'lambda_k2': 0.1 * jax.random.normal(ks[10], (DEPTH, DIFF_DH), f32),
        'final_norm_g': 1.0 + 0.01 * jax.random.normal(ks[11], (D_MODEL,), f32),
    }


def reference(x, meta_tokens, norm_g, w_in, w_out, ret_norm_g, diff_norm_g,
              lambda_q1, lambda_k1, lambda_q2, lambda_k2, final_norm_g):
    B, S, D = x.shape
    total = CHUNK + S
    Lp = -(-total // Q_BLOCK) * Q_BLOCK
    meta = jnp.broadcast_to(meta_tokens[None].astype(x.dtype), (B, N_META, D))
    h = jnp.concatenate([jnp.zeros((B, FRONT_PAD, D), x.dtype), meta, x,
                         jnp.zeros((B, Lp - total, D), x.dtype)], axis=1)
    pos = jnp.arange(Lp)
    valid = (pos >= FRONT_PAD) & (pos < total)
    for i in range(DEPTH):
        lam_init = 0.8 - 0.6 * float(np.exp(-0.3 * i))
        h = hybrid_layer(h, valid, norm_g[i], w_in[i], w_out[i], ret_norm_g[i], diff_norm_g[i],
                         lambda_q1[i], lambda_k1[i], lambda_q2[i], lambda_k2[i], lam_init)
    h = rms_norm(h, final_norm_g)
    return h[:, CHUNK:CHUNK + S]
```

```python
import contextlib
import numpy as np
import ml_dtypes
import concourse.bass as bass
import concourse.mybir as mybir
from concourse.bass_utils import run_bass_kernel_spmd

F32 = mybir.dt.float32
BF16 = mybir.dt.bfloat16
AF = mybir.ActivationFunctionType
ALU = mybir.AluOpType
AX = mybir.AxisListType

EPS = 1e-6


def _configure(d=2048, seq=2048, h=8, depth=2, ncores=8):
    g = globals()
    D = d
    SEQ = seq
    DEPTH = depth
    H = h
    T = -(-(64 + seq) // 128)
    L = T * 128
    KC = D // 128
    NO = D // 512
    HL = H // 2
    NHL = 2 * HL
    ML = NHL * 128
    NU = NHL + NO
    RGROUPS = [[2 * i, 2 * i + 1] for i in range(ncores // 2)]
    SPLIT = 6
    XW = [SPLIT * 128, (NHL - SPLIT) * 128]
    XB = [0, 2 * SPLIT * 128]
    XROWS = [512, 512]
    TGS = [(c, min(c + 512, L)) for c in range(0, L, 512)]
    NTG = len(TGS)
    NWBUF = 2
    LAM_INIT = [0.8 - 0.6 * float(np.exp(-0.3 * i)) for i in range(DEPTH)]
    NCORES = ncores
    C_DM = 0
    C_XI = C_DM + H * 128
    C_ZETA = C_XI + H * 64
    C_DEC = C_ZETA + H
    C_BK = C_DEC + H
    C_CORR = C_BK + H * 18
    C_SL8 = C_CORR + H * 128
    C_NG = C_SL8 + H
    C_EPS = C_NG + DEPTH * KC
    C_MH = C_EPS + 1
    C_END = C_MH + T
    B_ID = 0
    B_VAL = 128
    B_END = 128 + T
    g.update(locals())


_configure()


class Op:
    __slots__ = ("eng", "fn", "dma", "idx", "deps", "signal", "sig", "dsem", "dval", "waits", "tag", "nobar")

    def __init__(self, eng, fn, dma):
        self.eng = eng
        self.fn = fn
        self.dma = dma
        self.deps = []
        self.signal = False
        self.sig = 0
        self.dsem = None
        self.dval = 0
        self.waits = []


class Prog:
    ENGS = ("pe", "dve", "act", "pool", "sp")
    NDMASEM = 8
    SAME_ENG_DIST = 10 ** 9

    def __init__(self):
        self.eng_ops = {e: [] for e in self.ENGS}
        self.bar_pos = {}
        self.last_writer = {}
        self.readers = {}

    frozen = False
    tag = ""

    def add(self, eng, name, *args, reads=(), writes=(), dma=False, **kw):
        if self.frozen:
            return None
        nobar = bool(kw.pop("nobar", False))
        op = Op(eng, (name, args, kw), dma)
        op.tag = self.tag
        op.nobar = nobar
        writes = list(writes) + [k for k in reads if isinstance(k, str) and k.startswith("ps") and k[2:].isdigit()
                                 and k not in writes]
        deps = {}
        for k in reads:
            w = self.last_writer.get(k)
            if w is not None:
                deps[id(w)] = (w, "raw")
        for k in writes:
            w = self.last_writer.get(k)
            if w is not None and id(w) not in deps:
                deps[id(w)] = (w, "waw")
            r = self.readers.get(k)
            if r:
                for o in r[0].values():
                    if id(o) not in deps:
                        deps[id(o)] = (o, "war")
                for o in r[1]:
                    if id(o) not in deps:
                        deps[id(o)] = (o, "war")
        op.deps = list(deps.values())
        op.idx = len(self.eng_ops[eng])
        self.eng_ops[eng].append(op)
        for k in reads:
            r = self.readers.get(k)
            if r is None:
                r = ({}, [])
                self.readers[k] = r
            if dma:
                r[1].append(op)
            else:
                r[0][eng] = op
        for k in writes:
            self.last_writer[k] = op
            self.readers[k] = ({}, [])
        return op

    def barrier(self):
        lasts = {}
        dmas = []
        for e in self.ENGS:
            ops = self.eng_ops[e]
            comp = [o for o in ops if not o.dma]
            if comp:
                lasts[e] = comp[-1]
            dmas += [o for o in ops[self.bar_pos.get(e, 0):] if o.dma and not o.nobar]
            self.bar_pos[e] = len(ops)
        for e in self.ENGS:
            op = self.add(e, "nop")
            extra = [(p, "raw") for f, p in lasts.items() if f != e] + [(p, "raw") for p in dmas]
            op.deps = list(op.deps) + extra

    def resolve(self):
        for e in self.ENGS:
            for op in self.eng_ops[e]:
                need = []
                for (p, kind) in op.deps:
                    if p is op:
                        continue
                    if p.dma:
                        need.append(p)
                    elif p.eng != op.eng:
                        p.signal = True
                        need.append(p)
                    else:
                        if op.dma:
                            p.signal = True
                            need.append(p)
                        elif e != "pe" and (op.idx - p.idx) <= self.SAME_ENG_DIST:
                            p.signal = True
                            need.append(p)
                op.deps = need
        self.dma_count = {}
        self.cc_count = {}
        for e in self.ENGS:
            c = 0
            d = 0
            ncc = 0
            for op in self.eng_ops[e]:
                if op.dma and op.fn[0] == "collective_compute":
                    ncc += 1
                    op.dsem = (e, "cc")
                    op.dval = ncc
                elif op.dma:
                    op.dsem = (e, d % self.NDMASEM)
                    op.dval = 16 * (d // self.NDMASEM + 1)
                    d += 1
                elif op.signal:
                    c += 1
                    op.sig = c
            self.dma_count[e] = d
            self.cc_count[e] = ncc
        for e in self.ENGS:
            waited = {}
            for op in self.eng_ops[e]:
                ws = {}
                if op.dma and op.dval > 16 and op.dsem[1] != "cc":
                    ws[("d",) + op.dsem] = op.dval - 16
                for p in op.deps:
                    if p.dma:
                        key = ("d",) + p.dsem
                        val = p.dval
                    else:
                        key = ("c", p.eng)
                        val = p.sig
                    if val > ws.get(key, 0):
                        ws[key] = val
                op.waits = []
                for key, val in ws.items():
                    if val > waited.get(key, 0):
                        waited[key] = val
                        op.waits.append((key, val))

    def emit(self, nc):
        self.resolve()
        with contextlib.ExitStack() as st:
            sems = {}
            for e in self.ENGS:
                sems[("c", e)] = st.enter_context(nc.semaphore("c_" + e))
                for i in range(min(self.NDMASEM, self.dma_count[e])):
                    sems[("d", e, i)] = st.enter_context(nc.semaphore("d_%s_%d" % (e, i)))
                if self.cc_count[e]:
                    sems[("d", e, "cc")] = st.enter_context(nc.semaphore("cc_" + e))
            block = st.enter_context(nc.Block())

            def run(engname):
                def body(eng):
                    for op in self.eng_ops[engname]:
                        for key, val in op.waits:
                            eng.wait_ge(sems[key], val)
                        nm, a, kw = op.fn
                        ins = getattr(eng, nm)(*a, **kw)
                        if op.dma:
                            ins.then_inc(sems[("d",) + op.dsem], 1 if op.dsem[1] == "cc" else 16)
                        elif op.signal:
                            ins.then_inc(sems[("c", engname)], 1)
                    n = self.dma_count[engname]
                    for i in range(min(self.NDMASEM, n)):
                        uses = (n - i + self.NDMASEM - 1) // self.NDMASEM
                        eng.wait_ge(sems[("d", engname, i)], 16 * uses)
                    if self.cc_count[engname]:
                        eng.wait_ge(sems[("d", engname, "cc")], self.cc_count[engname])
                return body

            block.tensor(run("pe"))
            block.vector(run("dve"))
            block.scalar(run("act"))
            block.gpsimd(run("pool"))
            block.sync(run("sp"))


def build_program(layers=None, prologue=True, final=True, debug=None):
    if layers is None:
        layers = tuple(range(DEPTH))
    nc = bass.Bass("TRN2", target_bir_lowering=False)
    P = Prog()

    def ckpt(name):
        if debug is not None and debug == name:
            P.frozen = True
    NL = len(layers)

    x_d = nc.dram_tensor("x", [SEQ, D], F32, kind="ExternalInput").ap()
    meta_d = nc.dram_tensor("meta", [16, D], F32, kind="ExternalInput").ap()
    wst_d = nc.dram_tensor("wst", [DEPTH * NU, 128, KC * 512], F32, kind="ExternalInput").ap()
    ctab_d = nc.dram_tensor("ctab", [128, C_END], F32, kind="ExternalInput").ap()
    btab_d = nc.dram_tensor("btab", [128, B_END], BF16, kind="ExternalInput").ap()
    qa_d = nc.dram_tensor("qa", [2, L], BF16, kind="ExternalInput").ap()
    lq_d = nc.dram_tensor("lqt", [DEPTH, 128, 256], F32, kind="ExternalInput").ap()
    gtab_d = nc.dram_tensor("gtab", [DEPTH + 1, 128, D], F32, kind="ExternalInput").ap()
    y_d = nc.dram_tensor("y", [SEQ, D], F32, kind="ExternalOutput").ap()
    if prologue:
        h_d = nc.dram_tensor("hscr", [L, D], F32).ap()
    else:
        h_d = nc.dram_tensor("hio", [L, D], F32, kind="ExternalInput").ap()
    if not final:
        ho_d = nc.dram_tensor("hout", [L, D], F32, kind="ExternalOutput").ap()
    m_h = [nc.dram_tensor("mscr%d" % i, [L, XW[i]], BF16).ap() for i in range(2)]
    XCH = [[(c, min(c + XROWS[i], L)) for c in range(0, L, XROWS[i])] for i in range(2)]
    mf_d = [[nc.dram_tensor("mfull%d_%d" % (i, k), [2 * (c1 - c0), XW[i]], BF16).ap() for k, (c0, c1) in enumerate(XCH[i])]
            for i in range(2)]

    st = contextlib.ExitStack()

    def sb(name, shape, dt):
        return st.enter_context(nc.sbuf_tensor(name, shape, dt))

    actT = sb("actT", [128, KC, L], BF16)
    Wb = [sb("W%d" % i, [128, KC, 512], BF16) for i in range(NWBUF)]
    ctab = sb("ctab_s", [128, C_END], F32)
    btab = sb("btab_s", [128, B_END], BF16)
    Gbuf = sb("Gbuf", [128, D], F32)
    A0 = sb("A0", [128, L], BF16)
    A1 = sb("A1", [128, L], BF16)
    K0 = sb("K0", [128, L], BF16)
    K1 = sb("K1", [128, L], BF16)
    Vt0 = sb("Vt", [128, T, 130], BF16)
    GG0 = sb("GG", [128, T, 128], F32)
    Ot = sb("Ot", [128, T, 128], F32)
    KZ = sb("KZ", [128, T, 128], BF16)
    Sall = sb("Sall", [128, 2 * T, 128], BF16)
    S32 = [sb("S32_%d" % i, [128, 128], F32) for i in range(2)]
    sTm = [sb("sTm%d" % i, [128, 128], BF16) for i in range(2)]
    NSB = 2
    ET = [sb("ET%d" % i, [128, 512], BF16) for i in range(NSB)]
    accs = [sb("accs0", [128, 4, 130], F32)] * 2
    Ms = [sb("Ms0", [128, T, 128], BF16)] * 2
    RW = 2 * 2 * D + 2 * D
    Rg = sb("Rg", [128, RW], BF16)
    hb = [Rg[:, 0:2 * D].bitcast(F32), Rg[:, 2 * D:4 * D].bitcast(F32)]
    ub = [Rg[:, 4 * D:5 * D], Rg[:, 5 * D:6 * D]]
    o_gg = 0
    o_vt = o_gg + 2 * T * 128
    o_qr = ((o_vt + T * 130 + 31) // 32) * 32
    o_kr = o_qr + L
    assert o_kr + L <= RW
    GG1 = Rg[:, o_gg:o_gg + 2 * T * 128].bitcast(F32).rearrange("p (t c) -> p t c", c=128)
    Vt1 = Rg[:, o_vt:o_vt + T * 130].rearrange("p (t c) -> p t c", c=130)
    QR = Rg[:, o_qr:o_qr + L]
    KR = Rg[:, o_kr:o_kr + L]
    VtS = [Vt0, Vt1]
    GGS = [GG0, GG1]
    stat = sb("stat", [128, 8, T], F32)
    sm = sb("sm", [128, 64], F32)
    lamt = sb("lamt", [128, 8], F32)
    junk = sb("junk", [128, 128], BF16)
    o0 = [sb("o0_0", [128, 128], F32)] * 2
    tmp64 = Ot[:, 2, 0:64]
    stage = [sb("stage%d" % i, [128, 256], F32) for i in range(2)]

    ps = [st.enter_context(nc.psum_tensor("ps%d" % i, [128, 512], F32)) for i in range(8)]

    ident = btab[:, B_ID:B_ID + 128]

    P.add("sp", "dma_start", out=ctab[:, :], in_=ctab_d[:, :], writes=["ctab"], dma=True)
    P.add("sp", "dma_start", out=btab[:, :], in_=btab_d[:, :], writes=["btab"], dma=True)
    for nm, buf in (("A0", A0), ("A1", A1), ("K0", K0), ("K1", K1)):
        P.add("pool", "memset", buf[:, :], 0.0, writes=[nm])
    P.add("pool", "memset", Vt0[:, :, :], 0.0, writes=["Vt"])
    P.add("sp", "dma_start", out=A0[64:66, :], in_=qa_d[:, :], writes=["A0"], dma=True)
    P.add("sp", "dma_start", out=A1[0:2, :], in_=qa_d[:, :], writes=["A1"], dma=True)
    P.add("dve", "tensor_copy", out=Vt0[:, :, 128], in_=btab[:, B_VAL:B_VAL + T],
          reads=["btab"], writes=["Vt"])

    ckpt("const")
    if prologue:
        P.add("pool", "memset", hb[0][:, :], 0.0, writes=["hb0"])
        P.add("sp", "dma_start", out=h_d[0:48, :], in_=hb[0][0:48, :], reads=["hb0"], writes=["h0"], dma=True)
        P.add("sp", "dma_start", out=h_d[64 + SEQ:L, :], in_=hb[0][0:L - 64 - SEQ, :], reads=["hb0"], writes=["h%d" % (T - 1)], dma=True)
        P.add("sp", "dma_start", out=h_d[48:64, :], in_=meta_d[:, :], writes=["h0"], dma=True)
        P.add("sp", "dma_start", out=h_d[64:128, :], in_=x_d[0:64, :], writes=["h0"], dma=True)
        P.add("sp", "dma_start", out=h_d[(T - 1) * 128:64 + SEQ, :], in_=x_d[(T - 1) * 128 - 64:SEQ, :],
              writes=["h%d" % (T - 1)], dma=True)

    ckpt("prologue")

    def hkeys(t):
        return ["h%d" % t]

    def hsrc(first, t, c0=0, c1=D):
        if first and prologue and 1 <= t <= T - 2:
            return x_d[t * 128 - 64:(t + 1) * 128 - 64, c0:c1], []
        return h_d[t * 128:(t + 1) * 128, c0:c1], hkeys(t)

    wcount = [0]
    wunit_buf = {}

    def load_unit(gu):
        b = wcount[0] % NWBUF
        wcount[0] += 1
        wunit_buf[gu] = b
        P.add("pool", "dma_start",
            out=Wb[b][:, :, :], in_=wst_d[gu].rearrange("p (k c) -> p k c", k=KC),
            max_dma_last_dim=2048, writes=["W%d" % b], dma=True, nobar=True)
        return b

    unit_list = []
    for li in layers:
        for u in range(NU):
            unit_list.append(li * NU + u)
    upos = [0]

    def prefetch(n=1):
        for _ in range(n):
            if upos[0] < len(unit_list):
                load_unit(unit_list[upos[0]])
                upos[0] += 1

    prefetch(NWBUF - 1)
    ckpt("wload")

    psrot = {"a": 0, "b": 0, "c": 0}

    for lidx, li in enumerate(layers):
        P.add("sp", "dma_start", out=Gbuf[:, :], in_=gtab_d[li], writes=["Gbuf"], dma=True)
        lqs = Ot[:, 0:2, :].rearrange("p a b -> p (a b)")
        P.add("sp", "dma_start", out=lqs, in_=lq_d[li], reads=[("Ot", 0), ("Ot", 1)], writes=["lqs"], dma=True)
        P.add("dve", "tensor_tensor", out=tmp64[:, :], in0=lqs[:, 0:64], in1=lqs[:, 64:128], op=ALU.mult,
              reads=["lqs", ("Ot", 2)], writes=["tmp64", ("Ot", 2)])
        P.add("dve", "reduce_sum", out=lamt[:, 0:1], in_=tmp64[:, :], axis=AX.X, reads=["tmp64"], writes=["lamt0"])
        P.add("dve", "tensor_tensor", out=tmp64[:, :], in0=lqs[:, 128:192], in1=lqs[:, 192:256], op=ALU.mult,
              reads=["lqs", "lamt0", ("Ot", 2)], writes=["tmp64", ("Ot", 2)])
        P.add("dve", "reduce_sum", out=lamt[:, 1:2], in_=tmp64[:, :], axis=AX.X, reads=["tmp64"], writes=["lamt1"])
        P.add("act", "activation", out=lamt[:, 2:4], in_=lamt[:, 0:2], func=AF.Exp, reads=["lamt0", "lamt1"], writes=["lamt2"])
        P.add("dve", "scalar_tensor_tensor", out=lamt[:, 4:5], in0=lamt[:, 3:4], scalar=-LAM_INIT[li], in1=lamt[:, 2:3],
                                                              op0=ALU.add, op1=ALU.subtract,
              reads=["lamt2"], writes=["nlam"])

        P.tag = "phase1"
        NHB = 4
        hbs = [hb[0], hb[1], GG0[:, :, :].rearrange("p t c -> p (t c)")[:, 0:D],
               Sall[:, :, :].rearrange("p t c -> p (t c)")[:, 0:2 * D].bitcast(F32)]

        def p1_load(t):
            src, skeys = hsrc(lidx == 0, t)
            P.add("sp", "dma_start", out=hbs[t % NHB][:, :], in_=src,
                  reads=skeys, writes=["hb%d" % (t % NHB)], dma=True)

        def p1_a(t):
            hbuf = hbs[t % NHB]
            ubuf = ub[t % 2]
            hk = "hb%d" % (t % NHB)
            uk = "ub%d" % (t % 2)
            P.add("act", "activation", out=ubuf[:, :], in_=hbuf[:, :], func=AF.Square, accum_out=sm[:, t:t + 1],
                  reads=[hk], writes=[uk, ("sm", t)])
            P.add("act", "activation", out=sm[:, 32 + t:33 + t], in_=sm[:, t:t + 1], func=AF.Sqrt,
                  bias=ctab[:, C_EPS:C_EPS + 1], scale=1.0 / D,
                  reads=[("sm", t), "ctab"], writes=[("sm2", t)])
            P.add("dve", "reciprocal", out=sm[:, 32 + t:33 + t], in_=sm[:, 32 + t:33 + t],
                  reads=[("sm2", t)], writes=[("sm2", t)])

        def p1_b(t):
            hbuf = hbs[t % NHB]
            ubuf = ub[t % 2]
            hk = "hb%d" % (t % NHB)
            uk = "ub%d" % (t % 2)
            P.add("act", "activation", out=ubuf[:, :], in_=hbuf[:, :], func=AF.Copy, scale=sm[:, 32 + t:33 + t],
                  reads=[hk, ("sm2", t)], writes=[uk])
            for q4 in range(KC // 4):
                bank = psrot["a"] % 4
                psrot["a"] += 1
                pk = "ps%d" % bank
                pbf = ps[bank][:, :].bitcast(BF16)
                for i in range(4):
                    kc = q4 * 4 + i
                    P.add("pe", "transpose",
                          out=pbf[:, i * 128:(i + 1) * 128], in_=ubuf[:, kc * 128:(kc + 1) * 128], identity=ident,
                          reads=[uk, "btab"], writes=[pk])
                ng = C_NG + li * KC + q4 * 4
                P.add("dve", "tensor_tensor",
                      out=actT[:, q4 * 4:q4 * 4 + 4, t * 128:(t + 1) * 128],
                      in0=pbf[:, 0:512].rearrange("p (a b) -> p a b", b=128),
                      in1=ctab[:, ng:ng + 4].unsqueeze(2).broadcast_to([128, 4, 128]), op=ALU.mult,
                      reads=[pk, "ctab"], writes=[("aT", q4 * 4 + i, t) for i in range(4)])

        for t in range(min(NHB - 1, T)):
            p1_load(t)
        p1_a(0)
        for t in range(T):
            if t + NHB - 1 < T:
                p1_load(t + NHB - 1)
            if t + 1 < T:
                p1_a(t + 1)
            p1_b(t)

        ckpt("phase1")
        P.barrier()
        P.add("dve", "tensor_copy", out=Vt1[:, :, 128], in_=btab[:, B_VAL:B_VAL + T],
              reads=["btab"], writes=["Vt1c"])
        pj = [0]
        NH = NHL

        def proj_gen(hh):
            is_ret = (hh % 2 == 1)
            i = hh // 2
            par = hh % 2
            gu = li * NU + hh
            b = wunit_buf[gu]
            wb = Wb[b]
            wk = "W%d" % b
            prefetch(1)
            Vt = VtS[par]
            GGb = GGS[par]
            vinit = "Vt" if par == 0 else "Vt1c"

            def fm(col0, evac, tag):
                for tg, (c0, c1) in enumerate(TGS):
                    P.tag = tag
                    bank = pj[0] % 2
                    pj[0] += 1
                    pk = "ps%d" % bank
                    n = c1 - c0
                    tiles = range(c0 // 128, c1 // 128)
                    for kc in range(KC):
                        P.add("pe", "matmul",
                              ps[bank][:, 0:n], lhsT=wb[:, kc, col0:col0 + 128], rhs=actT[:, kc, c0:c1],
                              start=(kc == 0), stop=(kc == KC - 1),
                              reads=[wk] + [("aT", kc, t) for t in tiles], writes=[pk])
                        if kc == KC // 2 - 1:
                            yield
                            P.tag = tag
                    evac(tg, c0, c1, ps[bank], pk)
                    yield

            if is_ret:
                xi = ctab[:, C_XI + i * 64:C_XI + (i + 1) * 64]

                def evac_q(tg, c0, c1, pst, pk):
                    n = c1 - c0
                    P.add("dve", "tensor_tensor",
                          out=QR[:, c0:c1].rearrange("p (a b) -> p a b", b=64),
                          in0=pst[:, 0:n].rearrange("p (a b) -> p a b", b=64),
                          in1=xi.unsqueeze(1).broadcast_to([128, n // 64, 64]), op=ALU.mult,
                          reads=[pk, "ctab"], writes=[("QR", tg)])

                def evac_k(tg, c0, c1, pst, pk):
                    n = c1 - c0
                    P.add("act", "activation", out=KR[:, c0:c1], in_=pst[:, 0:n], func=AF.Copy, scale=float(128 ** -0.5),
                          reads=[pk], writes=[("KR", tg)])

                yield from fm(0, evac_q, "ret_fm")
                yield from fm(128, evac_k, "ret_fm")
            else:
                def evac_dq(tg, c0, c1, pst, pk):
                    n = c1 - c0
                    P.add("dve", "tensor_copy", out=A0[0:64, c0:c1], in_=pst[0:64, 0:n], reads=[pk, "A0"], writes=[("A0", tg)])
                    P.add("act", "activation", out=A1[64:128, c0:c1], in_=pst[64:128, 0:n], func=AF.Copy, reads=[pk, "A1"], writes=[("A1", tg)])

                def evac_dk(tg, c0, c1, pst, pk):
                    n = c1 - c0
                    P.add("dve", "tensor_copy", out=K0[0:64, c0:c1], in_=pst[0:64, 0:n], reads=[pk, "K0"], writes=[("K0", tg)])
                    P.add("act", "activation", out=K1[64:128, c0:c1], in_=pst[64:128, 0:n], func=AF.Copy, reads=[pk, "K1"], writes=[("K1", tg)])

                yield from fm(0, evac_dq, "diff_fm")
                yield from fm(128, evac_dk, "diff_fm")
                P.tag = "diff_fm"
                P.add("dve", "tensor_scalar", out=K0[64:66, :], in0=A0[64:66, :], scalar1=0.0,
                      scalar2=ctab[64:66, C_SL8 + i:C_SL8 + i + 1], op0=ALU.mult, op1=ALU.add,
                      reads=["A0", "ctab"], writes=["K0aug"])
                P.add("dve", "tensor_scalar", out=K1[0:2, :], in0=A1[0:2, :], scalar1=0.0,
                      scalar2=ctab[0:2, C_SL8 + i:C_SL8 + i + 1], op0=ALU.mult, op1=ALU.add,
                      reads=["A1", "ctab"], writes=["K1aug"])
                yield

            gcol = (0 if is_ret else D // 2) + i * 128
            for t in range(T):
                P.tag = "tmproj"
                bank = pj[0] % 2
                pj[0] += 1
                pk = "ps%d" % bank
                for kc in range(KC):
                    P.add("pe", "matmul",
                          ps[bank][:, 0:256], lhsT=actT[:, kc, t * 128:(t + 1) * 128], rhs=wb[:, kc, 256:512],
                          start=(kc == 0), stop=(kc == KC - 1),
                          reads=[wk, ("aT", kc, t)], writes=[pk])
                    if kc == KC // 2 - 1:
                        yield
                        P.tag = "tmproj"
                stg = stage[pj[0] % 2]
                sgk = "stage%d" % (pj[0] % 2)
                P.add("dve", "tensor_copy", out=stg[:, :], in_=ps[bank][:, 0:256], reads=[pk], writes=[sgk])
                P.add("pool", "tensor_copy", out=Vt[:, t, 0:128], in_=stg[:, 0:128],
                      reads=[sgk, vinit], writes=[("Vt", par, t)])
                P.add("act", "activation", out=GGb[:, t, :], in_=stg[:, 128:256], func=AF.Tanh, scale=0.5,
                      reads=[sgk], writes=[("GG", par, t)])
                P.add("dve", "scalar_tensor_tensor",
                      out=GGb[:, t, :], in0=GGb[:, t, :], scalar=1.0, in1=stg[:, 128:256], op0=ALU.add, op1=ALU.mult,
                      reads=[sgk, ("GG", par, t)], writes=[("GG", par, t)])
                if is_ret:
                    P.add("pool", "tensor_tensor", out=GGb[:, t, :], in0=GGb[:, t, :],
                          in1=Gbuf[:, gcol:gcol + 128], op=ALU.mult,
                          reads=[("GG", par, t), "Gbuf"], writes=[("GG", par, t)])
                else:
                    P.add("dve", "scalar_tensor_tensor",
                          out=GGb[:, t, :], in0=GGb[:, t, :], scalar=0.5 * (1.0 - LAM_INIT[li]), in1=Gbuf[:, gcol:gcol + 128],
                          op0=ALU.mult, op1=ALU.mult,
                          reads=[("GG", par, t), "Gbuf"], writes=[("GG", par, t)])
                yield

            if is_ret:
                for t in range(T):
                    P.tag = "ret_kz"
                    bank = pj[0] % 2
                    pj[0] += 1
                    pk = "ps%d" % bank
                    pbf = ps[bank][:, :].bitcast(BF16)
                    P.add("pe", "transpose", out=pbf[:, 0:128], in_=KR[:, t * 128:(t + 1) * 128], identity=ident,
                          reads=[("KR", t // 4), "btab"], writes=[pk])
                    P.add("dve", "tensor_scalar",
                          out=KZ[:, t, :], in0=pbf[:, 0:128], scalar1=ctab[:, C_ZETA + i:C_ZETA + i + 1], scalar2=None, op0=ALU.mult,
                          reads=[pk, "ctab"], writes=[("KZ", t)])
                    yield

        def mixer_gen(hh):
            is_ret = (hh % 2 == 1)
            i = hh // 2
            par = hh % 2
            Vt = VtS[par]
            GGb = GGS[par]
            vinit = "Vt" if par == 0 else "Vt1c"
            Mst = Ms[0]
            mk = "Ms0"
            if is_ret:
                P.tag = "ret_kv"
                P.add("pool", "memset", Sall[:, 0, :], 0.0, writes=[("Sall", 0)])
                P.add("pool", "memset", S32[0][:, :], 0.0, writes=["S32_0"])
                dec = ctab[:, C_DEC + i:C_DEC + i + 1]

                def emit_kv(t):
                    for n in (2 * t, 2 * t + 1):
                        if n >= 2 * T - 1:
                            continue
                        r0 = (n % 2) * 64
                        bank = 6 + (n % 2)
                        pk = "ps%d" % bank
                        P.add("pe", "matmul",
                              ps[bank][:, 0:128], lhsT=KZ[r0:r0 + 64, t, :], rhs=Vt[r0:r0 + 64, t, 0:128], start=True, stop=True,
                              reads=[("KZ", t), ("Vt", par, t)], writes=[pk])
                        so = S32[n % 2]
                        sn = S32[(n + 1) % 2]
                        P.add("dve", "scalar_tensor_tensor",
                              out=sn[:, :], in0=so[:, :], scalar=dec, in1=ps[bank][:, 0:128], op0=ALU.mult, op1=ALU.add,
                              reads=[pk, "S32_%d" % (n % 2), "ctab"], writes=["S32_%d" % ((n + 1) % 2)])
                        P.add("act", "activation", out=Sall[:, n + 1, :], in_=sn[:, :], func=AF.Copy,
                              reads=["S32_%d" % ((n + 1) % 2)], writes=[("Sall", n + 1)])

                def emit_sT(t):
                    bank = 4 + (t % 2)
                    pk = "ps%d" % bank
                    c0 = t * 128
                    P.add("pe", "matmul",
                          ps[bank][:, 0:128], lhsT=KR[:, c0:c0 + 128], rhs=QR[:, c0:c0 + 128], start=True, stop=True,
                          reads=[("KR", t // 4), ("QR", t // 4)], writes=[pk])
                    stm = sTm[t % 2]
                    sk = "sTm%d" % (t % 2)
                    P.add("dve", "tensor_tensor",
                          out=stm[:, :], in0=ps[bank][:, 0:128], in1=ctab[:, C_DM + i * 128:C_DM + (i + 1) * 128], op=ALU.mult,
                          reads=[pk, "ctab"], writes=[sk])

                def emit_out(t):
                    bank = 4 + (t % 2)
                    pk = "ps%d" % bank
                    c0 = t * 128
                    stm = sTm[t % 2]
                    sk = "sTm%d" % (t % 2)
                    P.add("pe", "matmul",
                          ps[bank][:, 256:384], lhsT=stm[:, :], rhs=Vt[:, t, 0:128], start=True, stop=False,
                          reads=[sk, ("Vt", par, t)], writes=[pk])
                    P.add("pe", "matmul",
                          ps[bank][0:64, 256:384], lhsT=QR[:, c0:c0 + 64], rhs=Sall[:, 2 * t, :], start=False, stop=True,
                          reads=[("QR", t // 4), ("Sall", 2 * t)], writes=[pk])
                    P.add("pe", "matmul",
                          ps[bank][64:128, 256:384], lhsT=QR[:, c0 + 64:c0 + 128], rhs=Sall[:, 2 * t + 1, :], start=False, stop=True,
                          reads=[("QR", t // 4), ("Sall", 2 * t + 1)], writes=[pk])
                    P.add("act", "activation", out=Ot[:, t, :], in_=ps[bank][:, 256:384], func=AF.Identity,
                          accum_out=stat[:, 0, t:t + 1],
                          reads=[pk], writes=[("Ot", t), ("st0", t)])
                    P.add("act", "activation", out=junk[:, 0:128], in_=Ot[:, t, :], func=AF.Square,
                          accum_out=stat[:, 1, t:t + 1],
                          reads=[("Ot", t)], writes=["junk", ("st1", t)])

                emit_kv(0)
                emit_sT(0)
                for t in range(T):
                    P.tag = "ret_kv"
                    if t + 1 < T:
                        emit_kv(t + 1)
                        emit_sT(t + 1)
                    P.tag = "ret_tile"
                    emit_out(t)
                    yield
                P.tag = "ret_norm"
                allst0 = [("st0", t) for t in range(T)]
                allst1 = [("st1", t) for t in range(T)]
                P.add("dve", "tensor_scalar", out=stat[:, 2, :], in0=stat[:, 0, :], scalar1=1.0 / 128, scalar2=None, op0=ALU.mult,
                      reads=allst0, writes=["mean"])
                P.add("dve", "tensor_tensor", out=stat[:, 3, :], in0=stat[:, 2, :], in1=stat[:, 2, :], op=ALU.mult,
                      reads=["mean"], writes=["msq"])
                P.add("dve", "scalar_tensor_tensor", out=stat[:, 4, :], in0=stat[:, 1, :], scalar=1.0 / 128, in1=stat[:, 3, :],
                      op0=ALU.mult, op1=ALU.subtract,
                      reads=allst1 + ["msq"], writes=["var"])
                P.add("dve", "tensor_scalar", out=stat[:, 4, :], in0=stat[:, 4, :], scalar1=EPS, scalar2=4.0, op0=ALU.add, op1=ALU.mult,
                      reads=["var"], writes=["var"])
                P.add("pool", "tensor_tensor", out=stat[:, 5, :], in0=stat[:, 4, :], in1=ctab[:, C_MH:C_MH + T], op=ALU.pow,
                      reads=["var", "ctab"], writes=["rstd"])
                yield
                for t in range(T):
                    P.tag = "ret_norm"
                    P.add("dve", "tensor_scalar", out=Ot[:, t, :], in0=Ot[:, t, :], scalar1=stat[:, 2, t:t + 1],
                          scalar2=stat[:, 5, t:t + 1], op0=ALU.subtract, op1=ALU.mult,
                          reads=[("Ot", t), "mean", "rstd"], writes=[("Ot", t)])
                    P.add("pool", "tensor_tensor", out=Mst[:, t, :], in0=Ot[:, t, :], in1=GGb[:, t, :], op=ALU.mult,
                          reads=[("Ot", t), ("GG", par, t)], writes=[mk])
                    yield
                fcol = i * 128
            else:
                corr = ctab[:, C_CORR + i * 128:C_CORR + (i + 1) * 128]
                ngrp = (T + 1) // 2
                its = []
                first_j = {}
                last_j = {}
                for g in range(ngrp):
                    qt = [t for t in (2 * g, 2 * g + 1) if t < T]
                    reg = list(range(qt[0]))
                    half = len(reg) // 2
                    order = reg[:half] + list(range(qt[0], qt[-1] + 1)) + reg[half:]
                    for j in order:
                        its.append((g, j, qt))
                        for t in qt:
                            if t >= j:
                                first_j.setdefault(t, j)
                                last_j[t] = j

                def emit_score(n):
                    g, j, qt = its[n]
                    cols = [t for t in qt if t >= j]
                    q0 = cols[0] * 128
                    nq = len(cols) * 128
                    bank = 2 + n % NSB
                    et = ET[n % NSB]
                    ek = "ET%d" % (n % NSB)
                    pk = "ps%d" % bank
                    tgq = sorted(set([cc // 4 for cc in cols]))
                    P.add("pe", "matmul",
                          ps[bank][:, 0:nq], lhsT=K0[:, j * 128:(j + 1) * 128], rhs=A0[:, q0:q0 + nq], start=True, stop=True,
                          reads=[("K0", j // 4), "K0aug", "K0", "A0"] + [("A0", x) for x in tgq], writes=[pk])
                    P.add("pe", "matmul",
                          ps[bank][:, 256:256 + nq], lhsT=K1[:, j * 128:(j + 1) * 128], rhs=A1[:, q0:q0 + nq], start=True, stop=True,
                          reads=[("K1", j // 4), "K1aug", "K1", "A1"] + [("A1", x) for x in tgq], writes=[pk])
                    dj = j - 2 * g + 16
                    bk = ctab[:, C_BK + i * 18 + dj:C_BK + i * 18 + dj + 1]
                    if nq == 256:
                        P.add("act", "activation",
                              out=et[:, :], in_=ps[bank][:, :], func=AF.Exp, bias=bk, scale=0.125,
                              reads=[pk, "ctab"], writes=[ek])
                    else:
                        P.add("act", "activation",
                              out=et[:, :].rearrange("p (c n) -> p c n", c=2)[:, :, 0:128],
                              in_=ps[bank][:, :].rearrange("p (c n) -> p c n", c=2)[:, :, 0:128],
                              func=AF.Exp, bias=bk, scale=0.125,
                              reads=[pk, "ctab"], writes=[ek])
                    if j == cols[0]:
                        etv = et[:, :].rearrange("p (c n) -> p c n", c=2)[:, :, 0:128]
                        P.add("dve", "tensor_tensor",
                              out=etv, in0=etv, in1=corr.unsqueeze(1).broadcast_to([128, 2, 128]), op=ALU.mult,
                              reads=[ek, "ctab"], writes=[ek])

                def emit_pv(n):
                    g, j, qt = its[n]
                    cols = [t for t in qt if t >= j]
                    et = ET[n % NSB]
                    ek = "ET%d" % (n % NSB)
                    for c in range(2):
                        for ci, t in enumerate(cols):
                            ab = 4 + c * 2 + qt.index(t)
                            P.add("pe", "matmul",
                                  ps[ab][:, 0:129], lhsT=et[:, c * 256 + ci * 128:c * 256 + (ci + 1) * 128], rhs=Vt[:, j, 0:129],
                                  start=(j == first_j[t]), stop=(j == last_j[t]),
                                  reads=[ek, ("Vt", par, j), vinit], writes=["ps%d" % ab])

                def emit_fin(g, qt):
                    acs = accs[0]
                    ak = "accs0"
                    for c in range(2):
                        for qi, t in enumerate(qt):
                            ab = 4 + c * 2 + qi
                            sl = c * 2 + qi
                            if c == 0:
                                P.add("act", "activation", out=acs[:, sl, 0:129], in_=ps[ab][:, 0:129], func=AF.Copy,
                                      reads=["ps%d" % ab], writes=[(ak, sl)])
                            else:
                                P.add("dve", "tensor_copy", out=acs[:, sl, 0:129], in_=ps[ab][:, 0:129],
                                      reads=["ps%d" % ab], writes=[(ak, sl)])
                    for qi, t in enumerate(qt):
                        s0 = qi
                        s1 = 2 + qi
                        o0t = o0[t % 2]
                        ok = "o0_0"
                        P.add("dve", "reciprocal", out=stat[:, 6, t:t + 1], in_=acs[:, s0, 128:129],
                              reads=[(ak, s0)], writes=[("rd0", t)])
                        P.add("dve", "reciprocal", out=stat[:, 7, t:t + 1], in_=acs[:, s1, 128:129],
                              reads=[(ak, s1)], writes=[("rd1", t)])
                        P.add("dve", "tensor_tensor", out=stat[:, 7, t:t + 1], in0=stat[:, 7, t:t + 1], in1=lamt[:, 4:5], op=ALU.mult,
                              reads=[("rd1", t), "nlam"], writes=[("rd1", t)])
                        P.add("act", "activation", out=o0t[:, :], in_=acs[:, s0, 0:128], func=AF.Copy, scale=stat[:, 6, t:t + 1],
                              reads=[(ak, s0), ("rd0", t)], writes=[ok])
                        P.add("dve", "scalar_tensor_tensor",
                              out=Ot[:, t, :], in0=acs[:, s1, 0:128], scalar=stat[:, 7, t:t + 1], in1=o0t[:, :], op0=ALU.mult, op1=ALU.add,
                              reads=[(ak, s1), ("rd1", t), ok], writes=[("Ot", t)])
                        P.add("act", "activation", out=junk[:, 0:128], in_=Ot[:, t, :], func=AF.Square,
                              accum_out=stat[:, 1, t:t + 1],
                              reads=[("Ot", t)], writes=["junk", ("st1", t)])

                LOOK = NSB - 1
                P.tag = "diff_attn"
                for n in range(min(LOOK, len(its))):
                    emit_score(n)
                for n in range(len(its)):
                    P.tag = "diff_attn"
                    if n + LOOK < len(its):
                        emit_score(n + LOOK)
                    emit_pv(n)
                    g, j, qt = its[n]
                    if n + 1 == len(its) or its[n + 1][0] != g:
                        emit_fin(g, qt)
                    yield
                P.tag = "diff_norm"
                allst1 = [("st1", t) for t in range(T)]
                P.add("dve", "tensor_scalar", out=stat[:, 4, :], in0=stat[:, 1, :], scalar1=1.0 / 128, scalar2=EPS, op0=ALU.mult, op1=ALU.add,
                      reads=allst1, writes=["var"])
                P.add("pool", "tensor_tensor", out=stat[:, 5, :], in0=stat[:, 4, :], in1=ctab[:, C_MH:C_MH + T], op=ALU.pow,
                      reads=["var", "ctab"], writes=["rstd"])
                yield
                for t in range(T):
                    P.tag = "diff_norm"
                    P.add("dve", "scalar_tensor_tensor",
                          out=Mst[:, t, :], in0=Ot[:, t, :], scalar=stat[:, 5, t:t + 1], in1=GGb[:, t, :], op0=ALU.mult, op1=ALU.mult,
                          reads=[("Ot", t), "rstd", ("GG", par, t)], writes=[mk])
                    yield
                fcol = ML // 2 + i * 128

            ckpt("head%d" % hh)
            P.add("sp", "dma_start",
                  out=m_h[0 if hh < SPLIT else 1][:, (hh if hh < SPLIT else hh - SPLIT) * 128:(hh if hh < SPLIT else hh - SPLIT) * 128 + 128].rearrange("(t p) c -> p t c", p=128), in_=Mst[:, :, :],
                  reads=[mk], writes=[("md", hh)], dma=True)
            yield

        def units_proj(hh):
            return 2 * (2 * NTG + T) + (T if hh % 2 == 1 else 1)

        def units_mix(hh):
            if hh % 2 == 1:
                return T + 1 + T + 1
            n_its = sum(min(2 * g + 1, T - 1) + 1 for g in range((T + 1) // 2))
            return n_its + 1 + T + 1

        def exchange(half):
            rk = [("md", hh) for hh in (range(SPLIT) if half == 0 else range(SPLIT, NHL))]
            for k, (c0, c1) in enumerate(XCH[half]):
                P.add("pool", "collective_compute", "AllGather", ALU.bypass, replica_groups=RGROUPS,
                      ins=[m_h[half][c0:c1, :]], outs=[mf_d[half][k][:, :]], reads=rk, writes=[("mfull", half, k)], dma=True)

        for _ in proj_gen(0):
            pass
        for hh in range(NH):
            gm = mixer_gen(hh)
            if hh + 1 < NH:
                gp = proj_gen(hh + 1)
                nm = units_mix(hh)
                npj = units_proj(hh + 1)
                dm = dp = 0
                gm_done = gp_done = False
                while not (gm_done and gp_done):
                    if hh == SPLIT and dm == 8:
                        exchange(0)
                        dm += 1
                    if not gm_done:
                        try:
                            next(gm)
                            dm += 1
                        except StopIteration:
                            gm_done = True
                    while not gp_done and (gm_done or dp * nm < (dm + 1) * npj):
                        try:
                            next(gp)
                            dp += 1
                        except StopIteration:
                            gp_done = True
            else:
                for _ in gm:
                    pass

        ckpt("heads")
        P.barrier()
        P.tag = "p3_tr"
        exchange(1)
        def p3_tr(t):
            P.tag = "p3_tr"
            ubuf = ub[t % 2]
            uk = "ub%d" % (t % 2)
            for half in range(2):
                tpc = XROWS[half] // 128
                P.add("sp", "dma_start", out=ubuf[:, XB[half]:XB[half] + 2 * XW[half]].rearrange("p (r c) -> p r c", r=2),
                      in_=mf_d[half][t // tpc].rearrange("(r l) c -> l r c", r=2)[(t % tpc) * 128:(t % tpc + 1) * 128, :, :],
                      reads=[("mfull", half, t // tpc)], writes=[uk if half == 0 else (uk, "B")], dma=True)
            for q4 in range(KC // 4):
                bank = 4 + psrot["a"] % 4
                psrot["a"] += 1
                pk = "ps%d" % bank
                pbf = ps[bank][:, :].bitcast(BF16)
                for ii in range(4):
                    kc = q4 * 4 + ii
                    P.add("pe", "transpose",
                        out=pbf[:, ii * 128:(ii + 1) * 128], in_=ubuf[:, kc * 128:(kc + 1) * 128], identity=ident,
                        reads=[uk, "btab"] + ([(uk, "B")] if (kc + 1) * 128 > XB[1] else []), writes=[pk])
                P.add("dve", "tensor_copy",
                    out=actT[:, q4 * 4:q4 * 4 + 4, t * 128:(t + 1) * 128],
                    in_=pbf[:, 0:512].rearrange("p (a b) -> p a b", b=128),
                    reads=[pk], writes=[("aT", q4 * 4 + ii, t) for ii in range(4)])
        last_layer = (lidx == NL - 1)
        P.tag = "p3_mm"
        p3 = [(jn, t) for jn in range(NO) for t in range(T)]
        NSL = 2 * NO

        def p3_load(n):
            jn, t = p3[n]
            slot = n % NSL
            hqb = hb[slot // NO][:, (slot % NO) * 512:(slot % NO + 1) * 512]
            src, skeys = hsrc(lidx == 0, t, jn * 512, (jn + 1) * 512)
            P.add("sp", "dma_start", out=hqb, in_=src,
                  reads=skeys + [("hq", t, jn), "hb%d" % (slot // NO)], writes=[("hbq", slot)], dma=True)

        PF = min(3, NSL - 1)
        for n in range(min(PF, len(p3))):
            p3_load(n)
        wb = None
        p3_tr(0)
        for n, (jn, t) in enumerate(p3):
            if jn == 0 and t + 1 < T:
                p3_tr(t + 1)
            P.tag = "p3_mm"
            if t == 0:
                gu = li * NU + NHL + jn
                b = wunit_buf[gu]
                wb = Wb[b]
                wk = "W%d" % b
                prefetch(1)
            if n + PF < len(p3):
                p3_load(n + PF)
            bank = psrot["b"] % 4
            psrot["b"] += 1
            pk = "ps%d" % bank
            slot = n % NSL
            hqb = hb[slot // NO][:, (slot % NO) * 512:(slot % NO + 1) * 512]
            hqk = ("hbq", slot)
            hbk = "hb%d" % (slot // NO)
            for kc in range(KC):
                P.add("pe", "matmul",
                      ps[bank][:, :], lhsT=actT[:, kc, t * 128:(t + 1) * 128], rhs=wb[:, kc, :],
                      start=(kc == 0), stop=(kc == KC - 1),
                      reads=[wk, ("aT", kc, t)], writes=[pk])
            P.add("dve", "tensor_tensor", out=hqb, in0=ps[bank][:, :], in1=hqb, op=ALU.add,
                  reads=[pk, hqk, hbk], writes=[hqk])
            P.add("act", "dma_start", out=h_d[t * 128:(t + 1) * 128, jn * 512:(jn + 1) * 512], in_=hqb,
                  reads=[hqk, hbk], writes=[("hq", t, jn), ("hw", t, jn)], dma=True)
        for t in range(T):
            P.add("sp", "nop", reads=[("hw", t, jn) for jn in range(NO)], writes=hkeys(t))

    ckpt("phase3")
    P.tag = "final"
    if final:
        P.barrier()
        P.add("sp", "dma_start", out=Gbuf[:, :], in_=gtab_d[DEPTH], writes=["Gbuf"], dma=True)
        flat = actT[:, :, :].rearrange("p k l -> p (k l)")
        NFB = 4
        fbs = [flat[:, i * 2 * D:(i + 1) * 2 * D].bitcast(F32) for i in range(NFB)]

        def fin_store(t):
            fb = fbs[t % NFB]
            fk = "fb%d" % (t % NFB)
            p0 = max(t * 128, 64)
            p1 = min((t + 1) * 128, 64 + SEQ)
            P.add("act", "dma_start",
                  out=y_d[p0 - 64:p1 - 64, :], in_=fb[p0 - t * 128:p1 - t * 128, :],
                  reads=[fk], writes=[("y", t)], dma=True)

        for t in range(T):
            fb = fbs[t % NFB]
            fk = "fb%d" % (t % NFB)
            P.add("sp", "dma_start", out=fb[:, :], in_=h_d[t * 128:(t + 1) * 128, :],
                  reads=hkeys(t), writes=[fk], dma=True)
            P.add("act", "activation", out=ub[t % 2][:, :], in_=fb[:, :], func=AF.Square, accum_out=sm[:, t:t + 1],
                  reads=[fk], writes=["ub%d" % (t % 2), ("sm", t)])
            P.add("act", "activation", out=sm[:, 32 + t:33 + t], in_=sm[:, t:t + 1], func=AF.Sqrt,
                  bias=ctab[:, C_EPS:C_EPS + 1], scale=1.0 / D,
                  reads=[("sm", t), "ctab"], writes=[("sm2", t)])
            P.add("dve", "reciprocal", out=sm[:, 32 + t:33 + t], in_=sm[:, 32 + t:33 + t],
                  reads=[("sm2", t)], writes=[("sm2", t)])
            P.add("dve", "scalar_tensor_tensor",
                  out=fb[:, :], in0=fb[:, :], scalar=sm[:, 32 + t:33 + t], in1=Gbuf[:, :], op0=ALU.mult, op1=ALU.mult,
                  reads=[fk, ("sm2", t), "Gbuf"], writes=[fk])
            if t >= 1:
                fin_store(t - 1)
        fin_store(T - 1)
    else:
        for t in range(T):
            hbuf = hb[t % 2]
            hk = "hb%d" % (t % 2)
            P.add("sp", "dma_start", out=hbuf[:, :], in_=h_d[t * 128:(t + 1) * 128, :],
                  reads=hkeys(t), writes=[hk], dma=True)
            P.add("sp", "dma_start", out=ho_d[t * 128:(t + 1) * 128, :], in_=hbuf[:, :],
                  reads=[hk], writes=[("ho", t)], dma=True)

    P.emit(nc)
    st.close()
    return nc, P


def _const_tables():
    ct = np.zeros((128, C_END), np.float32)
    p = np.arange(128)
    logg = np.log(1.0 - 2.0 ** (-5.0 - np.arange(H, dtype=np.float64)))
    slopes = 2.0 ** (-8.0 * np.arange(1, H + 1, dtype=np.float64) / H)
    tt = p % 64
    same = (p[:, None] // 64) == (p[None, :] // 64)
    for i in range(H):
        e = np.abs(tt[None, :] - tt[:, None]) - (tt[None, :] + 1.0)
        ct[:, C_DM + i * 128:C_DM + (i + 1) * 128] = np.where(same, np.exp(logg[i] * e), 0.0)
        ct[:, C_XI + i * 64:C_XI + (i + 1) * 64] = np.exp(logg[i] * (np.arange(64) + 1.0))[None, :]
        ct[:, C_ZETA + i] = np.exp(logg[i] * (63.0 - tt))
        ct[:, C_DEC + i] = np.exp(logg[i] * 64.0)
        for dj in range(18):
            ct[:, C_BK + i * 18 + dj] = slopes[i] * (p + 128.0 * (dj - 16))
        k = p[:, None]
        q = p[None, :]
        cr = np.where(k > q, np.exp(-2.0 * slopes[i] * (k - q)), 1.0)
        cr = np.where((k >= 64) & (q < 64), 0.0, cr)
        ct[:, C_CORR + i * 128:C_CORR + (i + 1) * 128] = cr
        ct[:, C_SL8 + i] = 8.0 * slopes[i]
    ct[:, C_EPS] = EPS
    ct[:, C_MH:C_MH + T] = -0.5
    bt = np.zeros((128, B_END), np.float32)
    bt[:, B_ID:B_ID + 128] = np.eye(128)
    pos = np.arange(T)[None, :] * 128 + p[:, None]
    bt[:, B_VAL:B_VAL + T] = ((pos >= 48) & (pos < 64 + SEQ)).astype(np.float32)
    qpos = np.arange(L)
    qa = np.stack([-(qpos % 128), -128.0 * ((qpos // 128) % 2)]).astype(np.float32)
    return ct, bt.astype(ml_dtypes.bfloat16), qa.astype(ml_dtypes.bfloat16)


def _prep_shared(norm_g, w_in, w_out, ret_norm_g, diff_norm_g, lq1, lk1, lq2, lk2, final_norm_g):
    ct, bt, qa = _const_tables()
    lqt = np.empty((DEPTH, 128, 256), np.float32)
    for li in range(DEPTH):
        ct[:, C_NG + li * KC:C_NG + (li + 1) * KC] = norm_g[li].reshape(KC, 128).T
        lqt[li] = np.concatenate([lq1[li], lk1[li], lq2[li], lk2[li]])[None, :]
    slots = ((C_DM, 128), (C_XI, 64), (C_ZETA, 1), (C_DEC, 1), (C_BK, 18), (C_CORR, 128), (C_SL8, 1))
    perm = np.empty(D, np.int64)
    for r in range(2):
        for hh in range(NHL):
            fam, il = 1 - hh % 2, hh // 2
            half, q = (0, hh) if hh < SPLIT else (1, hh - SPLIT)
            f0 = XB[half] + r * XW[half] + q * 128
            g0 = fam * (D // 2) + (2 * il + r) * 128
            perm[f0:f0 + 128] = np.arange(g0, g0 + 128)
    per_r = []
    for r in range(2):
        ctr = ct.copy()
        for il in range(HL):
            gi = 2 * il + r
            for (c0, w) in slots:
                ctr[:, c0 + il * w:c0 + (il + 1) * w] = ct[:, c0 + gi * w:c0 + (gi + 1) * w]
        gt = np.zeros((DEPTH + 1, 128, D), np.float32)
        for li in range(DEPTH):
            for il in range(HL):
                gi = 2 * il + r
                gt[li, :, il * 128:(il + 1) * 128] = ret_norm_g[li][None, gi * 128:(gi + 1) * 128]
                gt[li, :, D // 2 + il * 128:D // 2 + (il + 1) * 128] = diff_norm_g[li][None, gi * 128:(gi + 1) * 128]
        gt[DEPTH] = final_norm_g[None, :]
        wst = np.empty((DEPTH * NU, 128, KC, 512), np.float32)
        for li in range(DEPTH):
            w = w_in[li].reshape(KC, 128, 8, H, 128)
            for hh in range(NHL):
                gi = 2 * (hh // 2) + r
                fam0 = 0 if hh % 2 == 1 else 4
                blk = w[:, :, fam0:fam0 + 4, gi, :]
                wst[li * NU + hh] = blk.transpose(1, 0, 2, 3).reshape(128, KC, 512)
            wo = w_out[li][perm].reshape(KC, 128, NO, 512)
            for j in range(NO):
                wst[li * NU + NHL + j] = wo[:, :, j, :].transpose(1, 0, 2)
        per_r.append((ctr, gt, wst.reshape(DEPTH * NU, 128, KC * 512)))
    return per_r, bt, qa, lqt


_CACHE = {}
_DEBUG = None


def kernel(x, meta_tokens, norm_g, w_in, w_out, ret_norm_g, diff_norm_g,
           lambda_q1, lambda_k1, lambda_q2, lambda_k2, final_norm_g):
    f = lambda a: np.ascontiguousarray(np.asarray(a, dtype=np.float32))
    x = f(x)
    per_r, bt, qa, lqt = _prep_shared(f(norm_g), f(w_in), f(w_out), f(ret_norm_g), f(diff_norm_g),
                                      f(lambda_q1), f(lambda_k1), f(lambda_q2), f(lambda_k2), f(final_norm_g))
    meta = f(meta_tokens)
    if "nc" not in _CACHE:
        _CACHE["nc"] = build_program(debug=_DEBUG)[0]
    nc = _CACHE["nc"]
    in_maps = []
    for c in range(NCORES):
        ctr, gt, wst = per_r[c % 2]
        in_maps.append({"x": x[c // 2], "meta": meta, "wst": wst, "ctab": ctr, "btab": bt, "qa": qa, "gtab": gt, "lqt": lqt})
    res = run_bass_kernel_spmd(nc, in_maps, core_ids=list(range(NCORES)))
    out = np.stack([np.asarray(res.results[2 * b]["y"], dtype=np.float32) for b in range(NCORES // 2)], axis=0)
    return out
```

```python
import contextlib
import numpy as np
import ml_dtypes
import concourse.bass as bass
import concourse.mybir as mybir
from concourse.bass_utils import run_bass_kernel_spmd

F32 = mybir.dt.float32
BF16 = mybir.dt.bfloat16
AF = mybir.ActivationFunctionType
ALU = mybir.AluOpType
AX = mybir.AxisListType

EPS = 1e-6


def _configure(d=2048, seq=2048, h=8, depth=2, ncores=8):
    g = globals()
    D = d
    SEQ = seq
    DEPTH = depth
    H = h
    T = -(-(64 + seq) // 128)
    L = T * 128
    KC = D // 128
    NO = D // 512
    HL = H // 2
    NHL = 2 * HL
    ML = NHL * 128
    NU = NHL + NO
    RGROUPS = [[2 * i, 2 * i + 1] for i in range(ncores // 2)]
    SPLIT = 6
    XW = [SPLIT * 128, (NHL - SPLIT) * 128]
    XB = [0, 2 * SPLIT * 128]
    XROWS = [512, 1152]
    TGS = [(c, min(c + 512, L)) for c in range(0, L, 512)]
    NTG = len(TGS)
    NWBUF = 2
    LAM_INIT = [0.8 - 0.6 * float(np.exp(-0.3 * i)) for i in range(DEPTH)]
    NCORES = ncores
    C_DM = 0
    C_XI = C_DM + H * 128
    C_ZETA = C_XI + H * 64
    C_DEC = C_ZETA + H
    C_BK = C_DEC + H
    C_CORR = C_BK + H * 18
    C_SL8 = C_CORR + H * 128
    C_NG = C_SL8 + H
    C_EPS = C_NG + DEPTH * KC
    C_MH = C_EPS + 1
    C_END = C_MH + T
    B_ID = 0
    B_VAL = 128
    B_END = 128 + T
    g.update(locals())


_configure()


class Op:
    __slots__ = ("eng", "fn", "dma", "idx", "deps", "signal", "sig", "dsem", "dval", "waits", "tag", "nobar")

    def __init__(self, eng, fn, dma):
        self.eng = eng
        self.fn = fn
        self.dma = dma
        self.deps = []
        self.signal = False
        self.sig = 0
        self.dsem = None
        self.dval = 0
        self.waits = []


class Prog:
    ENGS = ("pe", "dve", "act", "pool", "sp")
    NDMASEM = 8
    SAME_ENG_DIST = 10 ** 9

    def __init__(self):
        self.eng_ops = {e: [] for e in self.ENGS}
        self.bar_pos = {}
        self.last_writer = {}
        self.readers = {}

    frozen = False
    tag = ""

    def add(self, eng, name, *args, reads=(), writes=(), dma=False, **kw):
        if self.frozen:
            return None
        nobar = bool(kw.pop("nobar", False))
        op = Op(eng, (name, args, kw), dma)
        op.tag = self.tag
        op.nobar = nobar
        writes = list(writes) + [k for k in reads if isinstance(k, str) and k.startswith("ps") and k[2:].isdigit()
                                 and k not in writes]
        deps = {}
        for k in reads:
            w = self.last_writer.get(k)
            if w is not None:
                deps[id(w)] = (w, "raw")
        for k in writes:
            w = self.last_writer.get(k)
            if w is not None and id(w) not in deps:
                deps[id(w)] = (w, "waw")
            r = self.readers.get(k)
            if r:
                for o in r[0].values():
                    if id(o) not in deps:
                        deps[id(o)] = (o, "war")
                for o in r[1]:
                    if id(o) not in deps:
                        deps[id(o)] = (o, "war")
        op.deps = list(deps.values())
        op.idx = len(self.eng_ops[eng])
        self.eng_ops[eng].append(op)
        for k in reads:
            r = self.readers.get(k)
            if r is None:
                r = ({}, [])
                self.readers[k] = r
            if dma:
                r[1].append(op)
            else:
                r[0][eng] = op
        for k in writes:
            self.last_writer[k] = op
            self.readers[k] = ({}, [])
        return op

    def barrier(self):
        lasts = {}
        dmas = []
        for e in self.ENGS:
            ops = self.eng_ops[e]
            comp = [o for o in ops if not o.dma]
            if comp:
                lasts[e] = comp[-1]
            dmas += [o for o in ops[self.bar_pos.get(e, 0):] if o.dma and not o.nobar]
            self.bar_pos[e] = len(ops)
        for e in self.ENGS:
            op = self.add(e, "nop")
            extra = [(p, "raw") for f, p in lasts.items() if f != e] + [(p, "raw") for p in dmas]
            op.deps = list(op.deps) + extra

    def resolve(self):
        for e in self.ENGS:
            for op in self.eng_ops[e]:
                need = []
                for (p, kind) in op.deps:
                    if p is op:
                        continue
                    if p.dma:
                        need.append(p)
                    elif p.eng != op.eng:
                        p.signal = True
                        need.append(p)
                    else:
                        if op.dma:
                            p.signal = True
                            need.append(p)
                        elif e != "pe" and (op.idx - p.idx) <= self.SAME_ENG_DIST:
                            p.signal = True
                            need.append(p)
                op.deps = need
        self.dma_count = {}
        self.cc_count = {}
        for e in self.ENGS:
            c = 0
            d = 0
            ncc = 0
            for op in self.eng_ops[e]:
                if op.dma and op.fn[0] == "collective_compute":
                    ncc += 1
                    op.dsem = (e, "cc")
                    op.dval = ncc
                elif op.dma:
                    op.dsem = (e, d % self.NDMASEM)
                    op.dval = 16 * (d // self.NDMASEM + 1)
                    d += 1
                elif op.signal:
                    c += 1
                    op.sig = c
            self.dma_count[e] = d
            self.cc_count[e] = ncc
        for e in self.ENGS:
            waited = {}
            for op in self.eng_ops[e]:
                ws = {}
                if op.dma and op.dval > 16 and op.dsem[1] != "cc":
                    ws[("d",) + op.dsem] = op.dval - 16
                for p in op.deps:
                    if p.dma:
                        key = ("d",) + p.dsem
                        val = p.dval
                    else:
                        key = ("c", p.eng)
                        val = p.sig
                    if val > ws.get(key, 0):
                        ws[key] = val
                op.waits = []
                for key, val in ws.items():
                    if val > waited.get(key, 0):
                        waited[key] = val
                        op.waits.append((key, val))

    def emit(self, nc):
        self.resolve()
        with contextlib.ExitStack() as st:
            sems = {}
            for e in self.ENGS:
                sems[("c", e)] = st.enter_context(nc.semaphore("c_" + e))
                for i in range(min(self.NDMASEM, self.dma_count[e])):
                    sems[("d", e, i)] = st.enter_context(nc.semaphore("d_%s_%d" % (e, i)))
                if self.cc_count[e]:
                    sems[("d", e, "cc")] = st.enter_context(nc.semaphore("cc_" + e))
            block = st.enter_context(nc.Block())

            def run(engname):
                def body(eng):
                    for op in self.eng_ops[engname]:
                        for key, val in op.waits:
                            eng.wait_ge(sems[key], val)
                        nm, a, kw = op.fn
                        ins = getattr(eng, nm)(*a, **kw)
                        if op.dma:
                            ins.then_inc(sems[("d",) + op.dsem], 1 if op.dsem[1] == "cc" else 16)
                        elif op.signal:
                            ins.then_inc(sems[("c", engname)], 1)
                    n = self.dma_count[engname]
                    for i in range(min(self.NDMASEM, n)):
                        uses = (n - i + self.NDMASEM - 1) // self.NDMASEM
                        eng.wait_ge(sems[("d", engname, i)], 16 * uses)
                    if self.cc_count[engname]:
                        eng.wait_ge(sems[("d", engname, "cc")], self.cc_count[engname])
                return body

            block.tensor(run("pe"))
            block.vector(run("dve"))
            block.scalar(run("act"))
            block.gpsimd(run("pool"))
            block.sync(run("sp"))


def build_program(layers=None, prologue=True, final=True, debug=None):
    if layers is None:
        layers = tuple(range(DEPTH))
    nc = bass.Bass("TRN2", target_bir_lowering=False)
    P = Prog()

    def ckpt(name):
        if debug is not None and debug == name:
            P.frozen = True
    NL = len(layers)

    x_d = nc.dram_tensor("x", [SEQ, D], F32, kind="ExternalInput").ap()
    meta_d = nc.dram_tensor("meta", [16, D], F32, kind="ExternalInput").ap()
    wst_d = nc.dram_tensor("wst", [DEPTH * NU, 128, KC * 512], F32, kind="ExternalInput").ap()
    ctab_d = nc.dram_tensor("ctab", [128, C_END], F32, kind="ExternalInput").ap()
    btab_d = nc.dram_tensor("btab", [128, B_END], BF16, kind="ExternalInput").ap()
    qa_d = nc.dram_tensor("qa", [2, L], BF16, kind="ExternalInput").ap()
    lq_d = nc.dram_tensor("lqt", [DEPTH, 128, 256], F32, kind="ExternalInput").ap()
    gtab_d = nc.dram_tensor("gtab", [DEPTH + 1, 128, D], F32, kind="ExternalInput").ap()
    y_d = nc.dram_tensor("y", [SEQ, D], F32, kind="ExternalOutput").ap()
    if prologue:
        h_d = nc.dram_tensor("hscr", [L, D], F32).ap()
    else:
        h_d = nc.dram_tensor("hio", [L, D], F32, kind="ExternalInput").ap()
    if not final:
        ho_d = nc.dram_tensor("hout", [L, D], F32, kind="ExternalOutput").ap()
    m_h = [nc.dram_tensor("mscr%d" % i, [L, XW[i]], BF16).ap() for i in range(2)]
    XCH = [[(c, min(c + XROWS[i], L)) for c in range(0, L, XROWS[i])] for i in range(2)]
    mf_d = [[nc.dram_tensor("mfull%d_%d" % (i, k), [2 * (c1 - c0), XW[i]], BF16).ap() for k, (c0, c1) in enumerate(XCH[i])]
            for i in range(2)]

    st = contextlib.ExitStack()

    def sb(name, shape, dt):
        return st.enter_context(nc.sbuf_tensor(name, shape, dt))

    actT = sb("actT", [128, KC, L], BF16)
    Wb = [sb("W%d" % i, [128, KC, 512], BF16) for i in range(NWBUF)]
    ctab = sb("ctab_s", [128, C_END], F32)
    btab = sb("btab_s", [128, B_END], BF16)
    Gbuf = sb("Gbuf", [128, D], F32)
    A0 = sb("A0", [128, L], BF16)
    A1 = sb("A1", [128, L], BF16)
    K0 = sb("K0", [128, L], BF16)
    K1 = sb("K1", [128, L], BF16)
    Vt0 = sb("Vt", [128, T, 130], BF16)
    GG0 = sb("GG", [128, T, 128], F32)
    Ot = sb("Ot", [128, T, 128], F32)
    KZ = sb("KZ", [128, T, 128], BF16)
    Sall = sb("Sall", [128, 2 * T, 128], BF16)
    S32 = [sb("S32_%d" % i, [128, 128], F32) for i in range(2)]
    sTm = [sb("sTm%d" % i, [128, 128], BF16) for i in range(2)]
    NSB = 2
    ET = [sb("ET%d" % i, [128, 512], BF16) for i in range(NSB)]
    accs = [sb("accs0", [128, 4, 130], F32)] * 2
    Ms = [sb("Ms0", [128, T, 128], BF16)] * 2
    RW = 2 * 2 * D + 2 * D
    Rg = sb("Rg", [128, RW], BF16)
    hb = [Rg[:, 0:2 * D].bitcast(F32), Rg[:, 2 * D:4 * D].bitcast(F32)]
    ub = [Rg[:, 4 * D:5 * D], Rg[:, 5 * D:6 * D]]
    o_gg = 0
    o_vt = o_gg + 2 * T * 128
    o_qr = ((o_vt + T * 130 + 31) // 32) * 32
    o_kr = o_qr + L
    assert o_kr + L <= RW
    GG1 = Rg[:, o_gg:o_gg + 2 * T * 128].bitcast(F32).rearrange("p (t c) -> p t c", c=128)
    Vt1 = Rg[:, o_vt:o_vt + T * 130].rearrange("p (t c) -> p t c", c=130)
    QR = Rg[:, o_qr:o_qr + L]
    KR = Rg[:, o_kr:o_kr + L]
    VtS = [Vt0, Vt1]
    GGS = [GG0, GG1]
    stat = sb("stat", [128, 8, T], F32)
    sm = sb("sm", [128, 64], F32)
    lamt = sb("lamt", [128, 8], F32)
    junk = sb("junk", [128, 128], BF16)
    o0 = [sb("o0_0", [128, 128], F32)] * 2
    tmp64 = Ot[:, 2, 0:64]
    stage = [sb("stage%d" % i, [128, 256], F32) for i in range(2)]

    ps = [st.enter_context(nc.psum_tensor("ps%d" % i, [128, 512], F32)) for i in range(8)]

    ident = btab[:, B_ID:B_ID + 128]

    P.add("sp", "dma_start", out=ctab[:, :], in_=ctab_d[:, :], writes=["ctab"], dma=True)
    P.add("sp", "dma_start", out=btab[:, :], in_=btab_d[:, :], writes=["btab"], dma=True)
    for nm, buf in (("A0", A0), ("A1", A1), ("K0", K0), ("K1", K1)):
        P.add("pool", "memset", buf[:, :], 0.0, writes=[nm])
    P.add("pool", "memset", Vt0[:, :, :], 0.0, writes=["Vt"])
    P.add("sp", "dma_start", out=A0[64:66, :], in_=qa_d[:, :], writes=["A0"], dma=True)
    P.add("sp", "dma_start", out=A1[0:2, :], in_=qa_d[:, :], writes=["A1"], dma=True)
    P.add("dve", "tensor_copy", out=Vt0[:, :, 128], in_=btab[:, B_VAL:B_VAL + T],
          reads=["btab"], writes=["Vt"])

    ckpt("const")
    if prologue:
        P.add("pool", "memset", hb[0][:, :], 0.0, writes=["hb0"])
        P.add("sp", "dma_start", out=h_d[0:48, :], in_=hb[0][0:48, :], reads=["hb0"], writes=["h0"], dma=True)
        P.add("sp", "dma_start", out=h_d[64 + SEQ:L, :], in_=hb[0][0:L - 64 - SEQ, :], reads=["hb0"], writes=["h%d" % (T - 1)], dma=True)
        P.add("sp", "dma_start", out=h_d[48:64, :], in_=meta_d[:, :], writes=["h0"], dma=True)
        P.add("sp", "dma_start", out=h_d[64:128, :], in_=x_d[0:64, :], writes=["h0"], dma=True)
        P.add("sp", "dma_start", out=h_d[(T - 1) * 128:64 + SEQ, :], in_=x_d[(T - 1) * 128 - 64:SEQ, :],
              writes=["h%d" % (T - 1)], dma=True)

    ckpt("prologue")

    def hkeys(t):
        return ["h%d" % t]

    def hsrc(first, t, c0=0, c1=D):
        if first and prologue and 1 <= t <= T - 2:
            return x_d[t * 128 - 64:(t + 1) * 128 - 64, c0:c1], []
        return h_d[t * 128:(t + 1) * 128, c0:c1], hkeys(t)

    wcount = [0]
    wunit_buf = {}

    def load_unit(gu):
        b = wcount[0] % NWBUF
        wcount[0] += 1
        wunit_buf[gu] = b
        P.add("pool", "dma_start",
            out=Wb[b][:, :, :], in_=wst_d[gu].rearrange("p (k c) -> p k c", k=KC),
            max_dma_last_dim=2048, writes=["W%d" % b], dma=True, nobar=True)
        return b

    unit_list = []
    for li in layers:
        for u in range(NU):
            unit_list.append(li * NU + u)
    upos = [0]

    def prefetch(n=1):
        for _ in range(n):
            if upos[0] < len(unit_list):
                load_unit(unit_list[upos[0]])
                upos[0] += 1

    prefetch(NWBUF - 1)
    ckpt("wload")

    psrot = {"a": 0, "b": 0, "c": 0}

    for lidx, li in enumerate(layers):
        P.add("sp", "dma_start", out=Gbuf[:, :], in_=gtab_d[li], writes=["Gbuf"], dma=True)
        lqs = Ot[:, 0:2, :].rearrange("p a b -> p (a b)")
        P.add("sp", "dma_start", out=lqs, in_=lq_d[li], reads=[("Ot", 0), ("Ot", 1)], writes=["lqs"], dma=True)
        P.add("dve", "tensor_tensor", out=tmp64[:, :], in0=lqs[:, 0:64], in1=lqs[:, 64:128], op=ALU.mult,
              reads=["lqs", ("Ot", 2)], writes=["tmp64", ("Ot", 2)])
        P.add("dve", "reduce_sum", out=lamt[:, 0:1], in_=tmp64[:, :], axis=AX.X, reads=["tmp64"], writes=["lamt0"])
        P.add("dve", "tensor_tensor", out=tmp64[:, :], in0=lqs[:, 128:192], in1=lqs[:, 192:256], op=ALU.mult,
              reads=["lqs", "lamt0", ("Ot", 2)], writes=["tmp64", ("Ot", 2)])
        P.add("dve", "reduce_sum", out=lamt[:, 1:2], in_=tmp64[:, :], axis=AX.X, reads=["tmp64"], writes=["lamt1"])
        P.add("act", "activation", out=lamt[:, 2:4], in_=lamt[:, 0:2], func=AF.Exp, reads=["lamt0", "lamt1"], writes=["lamt2"])
        P.add("dve", "scalar_tensor_tensor", out=lamt[:, 4:5], in0=lamt[:, 3:4], scalar=-LAM_INIT[li], in1=lamt[:, 2:3],
                                                              op0=ALU.add, op1=ALU.subtract,
              reads=["lamt2"], writes=["nlam"])

        P.tag = "phase1"
        NHB = 4
        hbs = [hb[0], hb[1], GG0[:, :, :].rearrange("p t c -> p (t c)")[:, 0:D],
               Sall[:, :, :].rearrange("p t c -> p (t c)")[:, 0:2 * D].bitcast(F32)]

        def p1_load(t):
            src, skeys = hsrc(lidx == 0, t)
            P.add("sp", "dma_start", out=hbs[t % NHB][:, :], in_=src,
                  reads=skeys, writes=["hb%d" % (t % NHB)], dma=True)

        def p1_a(t):
            hbuf = hbs[t % NHB]
            ubuf = ub[t % 2]
            hk = "hb%d" % (t % NHB)
            uk = "ub%d" % (t % 2)
            P.add("act", "activation", out=ubuf[:, :], in_=hbuf[:, :], func=AF.Square, accum_out=sm[:, t:t + 1],
                  reads=[hk], writes=[uk, ("sm", t)])
            P.add("act", "activation", out=sm[:, 32 + t:33 + t], in_=sm[:, t:t + 1], func=AF.Sqrt,
                  bias=ctab[:, C_EPS:C_EPS + 1], scale=1.0 / D,
                  reads=[("sm", t), "ctab"], writes=[("sm2", t)])
            P.add("dve", "reciprocal", out=sm[:, 32 + t:33 + t], in_=sm[:, 32 + t:33 + t],
                  reads=[("sm2", t)], writes=[("sm2", t)])

        def p1_b(t):
            hbuf = hbs[t % NHB]
            ubuf = ub[t % 2]
            hk = "hb%d" % (t % NHB)
            uk = "ub%d" % (t % 2)
            P.add("act", "activation", out=ubuf[:, :], in_=hbuf[:, :], func=AF.Copy, scale=sm[:, 32 + t:33 + t],
                  reads=[hk, ("sm2", t)], writes=[uk])
            for q4 in range(KC // 4):
                bank = psrot["a"] % 4
                psrot["a"] += 1
                pk = "ps%d" % bank
                pbf = ps[bank][:, :].bitcast(BF16)
                for i in range(4):
                    kc = q4 * 4 + i
                    P.add("pe", "transpose",
                          out=pbf[:, i * 128:(i + 1) * 128], in_=ubuf[:, kc * 128:(kc + 1) * 128], identity=ident,
                          reads=[uk, "btab"], writes=[pk])
                ng = C_NG + li * KC + q4 * 4
                P.add("dve", "tensor_tensor",
                      out=actT[:, q4 * 4:q4 * 4 + 4, t * 128:(t + 1) * 128],
                      in0=pbf[:, 0:512].rearrange("p (a b) -> p a b", b=128),
                      in1=ctab[:, ng:ng + 4].unsqueeze(2).broadcast_to([128, 4, 128]), op=ALU.mult,
                      reads=[pk, "ctab"], writes=[("aT", q4 * 4 + i, t) for i in range(4)])

        for t in range(min(NHB - 1, T)):
            p1_load(t)
        p1_a(0)
        for t in range(T):
            if t + NHB - 1 < T:
                p1_load(t + NHB - 1)
            if t + 1 < T:
                p1_a(t + 1)
            p1_b(t)

        ckpt("phase1")
        P.barrier()
        P.add("dve", "tensor_copy", out=Vt1[:, :, 128], in_=btab[:, B_VAL:B_VAL + T],
              reads=["btab"], writes=["Vt1c"])
        pj = [0]
        NH = NHL

        def proj_gen(hh):
            is_ret = (hh % 2 == 1)
            i = hh // 2
            par = hh % 2
            gu = li * NU + hh
            b = wunit_buf[gu]
            wb = Wb[b]
            wk = "W%d" % b
            prefetch(1)
            Vt = VtS[par]
            GGb = GGS[par]
            vinit = "Vt" if par == 0 else "Vt1c"

            def fm(col0, evac, tag):
                for tg, (c0, c1) in enumerate(TGS):
                    P.tag = tag
                    bank = pj[0] % 2
                    pj[0] += 1
                    pk = "ps%d" % bank
                    n = c1 - c0
                    tiles = range(c0 // 128, c1 // 128)
                    for kc in range(KC):
                        P.add("pe", "matmul",
                              ps[bank][:, 0:n], lhsT=wb[:, kc, col0:col0 + 128], rhs=actT[:, kc, c0:c1],
                              start=(kc == 0), stop=(kc == KC - 1),
                              reads=[wk] + [("aT", kc, t) for t in tiles], writes=[pk])
                        if kc == KC // 2 - 1:
                            yield
                            P.tag = tag
                    evac(tg, c0, c1, ps[bank], pk)
                    yield

            if is_ret:
                xi = ctab[:, C_XI + i * 64:C_XI + (i + 1) * 64]

                def evac_q(tg, c0, c1, pst, pk):
                    n = c1 - c0
                    P.add("dve", "tensor_tensor",
                          out=QR[:, c0:c1].rearrange("p (a b) -> p a b", b=64),
                          in0=pst[:, 0:n].rearrange("p (a b) -> p a b", b=64),
                          in1=xi.unsqueeze(1).broadcast_to([128, n // 64, 64]), op=ALU.mult,
                          reads=[pk, "ctab"], writes=[("QR", tg)])

                def evac_k(tg, c0, c1, pst, pk):
                    n = c1 - c0
                    P.add("act", "activation", out=KR[:, c0:c1], in_=pst[:, 0:n], func=AF.Copy, scale=float(128 ** -0.5),
                          reads=[pk], writes=[("KR", tg)])

                yield from fm(0, evac_q, "ret_fm")
                yield from fm(128, evac_k, "ret_fm")
            else:
                def evac_dq(tg, c0, c1, pst, pk):
                    n = c1 - c0
                    P.add("dve", "tensor_copy", out=A0[0:64, c0:c1], in_=pst[0:64, 0:n], reads=[pk, "A0"], writes=[("A0", tg)])
                    P.add("act", "activation", out=A1[64:128, c0:c1], in_=pst[64:128, 0:n], func=AF.Copy, reads=[pk, "A1"], writes=[("A1", tg)])

                def evac_dk(tg, c0, c1, pst, pk):
                    n = c1 - c0
                    P.add("dve", "tensor_copy", out=K0[0:64, c0:c1], in_=pst[0:64, 0:n], reads=[pk, "K0"], writes=[("K0", tg)])
                    P.add("act", "activation", out=K1[64:128, c0:c1], in_=pst[64:128, 0:n], func=AF.Copy, reads=[pk, "K1"], writes=[("K1", tg)])

                yield from fm(0, evac_dq, "diff_fm")
                yield from fm(128, evac_dk, "diff_fm")
                P.tag = "diff_fm"
                P.add("dve", "tensor_scalar", out=K0[64:66, :], in0=A0[64:66, :], scalar1=0.0,
                      scalar2=ctab[64:66, C_SL8 + i:C_SL8 + i + 1], op0=ALU.mult, op1=ALU.add,
                      reads=["A0", "ctab"], writes=["K0aug"])
                P.add("dve", "tensor_scalar", out=K1[0:2, :], in0=A1[0:2, :], scalar1=0.0,
                      scalar2=ctab[0:2, C_SL8 + i:C_SL8 + i + 1], op0=ALU.mult, op1=ALU.add,
                      reads=["A1", "ctab"], writes=["K1aug"])
                yield

            gcol = (0 if is_ret else D // 2) + i * 128
            for t in range(T):
                P.tag = "tmproj"
                bank = pj[0] % 2
                pj[0] += 1
                pk = "ps%d" % bank
                for kc in range(KC):
                    P.add("pe", "matmul",
                          ps[bank][:, 0:256], lhsT=actT[:, kc, t * 128:(t + 1) * 128], rhs=wb[:, kc, 256:512],
                          start=(kc == 0), stop=(kc == KC - 1),
                          reads=[wk, ("aT", kc, t)], writes=[pk])
                    if kc == KC // 2 - 1:
                        yield
                        P.tag = "tmproj"
                stg = stage[pj[0] % 2]
                sgk = "stage%d" % (pj[0] % 2)
                P.add("dve", "tensor_copy", out=stg[:, :], in_=ps[bank][:, 0:256], reads=[pk], writes=[sgk])
                P.add("pool", "tensor_copy", out=Vt[:, t, 0:128], in_=stg[:, 0:128],
                      reads=[sgk, vinit], writes=[("Vt", par, t)])
                P.add("act", "activation", out=GGb[:, t, :], in_=stg[:, 128:256], func=AF.Tanh, scale=0.5,
                      reads=[sgk], writes=[("GG", par, t)])
                P.add("dve", "scalar_tensor_tensor",
                      out=GGb[:, t, :], in0=GGb[:, t, :], scalar=1.0, in1=stg[:, 128:256], op0=ALU.add, op1=ALU.mult,
                      reads=[sgk, ("GG", par, t)], writes=[("GG", par, t)])
                if is_ret:
                    P.add("pool", "tensor_tensor", out=GGb[:, t, :], in0=GGb[:, t, :],
                          in1=Gbuf[:, gcol:gcol + 128], op=ALU.mult,
                          reads=[("GG", par, t), "Gbuf"], writes=[("GG", par, t)])
                else:
                    P.add("dve", "scalar_tensor_tensor",
                          out=GGb[:, t, :], in0=GGb[:, t, :], scalar=0.5 * (1.0 - LAM_INIT[li]), in1=Gbuf[:, gcol:gcol + 128],
                          op0=ALU.mult, op1=ALU.mult,
                          reads=[("GG", par, t), "Gbuf"], writes=[("GG", par, t)])
                yield

            if is_ret:
                for t in range(T):
                    P.tag = "ret_kz"
                    bank = pj[0] % 2
                    pj[0] += 1
                    pk = "ps%d" % bank
                    pbf = ps[bank][:, :].bitcast(BF16)
                    P.add("pe", "transpose", out=pbf[:, 0:128], in_=KR[:, t * 128:(t + 1) * 128], identity=ident,
                          reads=[("KR", t // 4), "btab"], writes=[pk])
                    P.add("dve", "tensor_scalar",
                          out=KZ[:, t, :], in0=pbf[:, 0:128], scalar1=ctab[:, C_ZETA + i:C_ZETA + i + 1], scalar2=None, op0=ALU.mult,
                          reads=[pk, "ctab"], writes=[("KZ", t)])
                    yield

        def mixer_gen(hh):
            is_ret = (hh % 2 == 1)
            i = hh // 2
            par = hh % 2
            Vt = VtS[par]
            GGb = GGS[par]
            vinit = "Vt" if par == 0 else "Vt1c"
            Mst = Ms[0]
            mk = "Ms0"
            if is_ret:
                P.tag = "ret_kv"
                P.add("pool", "memset", Sall[:, 0, :], 0.0, writes=[("Sall", 0)])
                P.add("pool", "memset", S32[0][:, :], 0.0, writes=["S32_0"])
                dec = ctab[:, C_DEC + i:C_DEC + i + 1]

                def emit_kv(t):
                    for n in (2 * t, 2 * t + 1):
                        if n >= 2 * T - 1:
                            continue
                        r0 = (n % 2) * 64
                        bank = 6 + (n % 2)
                        pk = "ps%d" % bank
                        P.add("pe", "matmul",
                              ps[bank][:, 0:128], lhsT=KZ[r0:r0 + 64, t, :], rhs=Vt[r0:r0 + 64, t, 0:128], start=True, stop=True,
                              reads=[("KZ", t), ("Vt", par, t)], writes=[pk])
                        so = S32[n % 2]
                        sn = S32[(n + 1) % 2]
                        P.add("dve", "scalar_tensor_tensor",
                              out=sn[:, :], in0=so[:, :], scalar=dec, in1=ps[bank][:, 0:128], op0=ALU.mult, op1=ALU.add,
                              reads=[pk, "S32_%d" % (n % 2), "ctab"], writes=["S32_%d" % ((n + 1) % 2)])
                        P.add("act", "activation", out=Sall[:, n + 1, :], in_=sn[:, :], func=AF.Copy,
                              reads=["S32_%d" % ((n + 1) % 2)], writes=[("Sall", n + 1)])

                def emit_sT(t):
                    bank = 4 + (t % 2)
                    pk = "ps%d" % bank
                    c0 = t * 128
                    P.add("pe", "matmul",
                          ps[bank][:, 0:128], lhsT=KR[:, c0:c0 + 128], rhs=QR[:, c0:c0 + 128], start=True, stop=True,
                          reads=[("KR", t // 4), ("QR", t // 4)], writes=[pk])
                    stm = sTm[t % 2]
                    sk = "sTm%d" % (t % 2)
                    P.add("dve", "tensor_tensor",
                          out=stm[:, :], in0=ps[bank][:, 0:128], in1=ctab[:, C_DM + i * 128:C_DM + (i + 1) * 128], op=ALU.mult,
                          reads=[pk, "ctab"], writes=[sk])

                def emit_out(t):
                    bank = 4 + (t % 2)
                    pk = "ps%d" % bank
                    c0 = t * 128
                    stm = sTm[t % 2]
                    sk = "sTm%d" % (t % 2)
                    P.add("pe", "matmul",
                          ps[bank][:, 256:384], lhsT=stm[:, :], rhs=Vt[:, t, 0:128], start=True, stop=False,
                          reads=[sk, ("Vt", par, t)], writes=[pk])
                    P.add("pe", "matmul",
                          ps[bank][0:64, 256:384], lhsT=QR[:, c0:c0 + 64], rhs=Sall[:, 2 * t, :], start=False, stop=True,
                          reads=[("QR", t // 4), ("Sall", 2 * t)], writes=[pk])
                    P.add("pe", "matmul",
                          ps[bank][64:128, 256:384], lhsT=QR[:, c0 + 64:c0 + 128], rhs=Sall[:, 2 * t + 1, :], start=False, stop=True,
                          reads=[("QR", t // 4), ("Sall", 2 * t + 1)], writes=[pk])
                    P.add("act", "activation", out=Ot[:, t, :], in_=ps[bank][:, 256:384], func=AF.Identity,
                          accum_out=stat[:, 0, t:t + 1],
                          reads=[pk], writes=[("Ot", t), ("st0", t)])
                    P.add("act", "activation", out=junk[:, 0:128], in_=Ot[:, t, :], func=AF.Square,
                          accum_out=stat[:, 1, t:t + 1],
                          reads=[("Ot", t)], writes=["junk", ("st1", t)])

                emit_kv(0)
                emit_sT(0)
                for t in range(T):
                    P.tag = "ret_kv"
                    if t + 1 < T:
                        emit_kv(t + 1)
                        emit_sT(t + 1)
                    P.tag = "ret_tile"
                    emit_out(t)
                    yield
                P.tag = "ret_norm"
                allst0 = [("st0", t) for t in range(T)]
                allst1 = [("st1", t) for t in range(T)]
                P.add("dve", "tensor_scalar", out=stat[:, 2, :], in0=stat[:, 0, :], scalar1=1.0 / 128, scalar2=None, op0=ALU.mult,
                      reads=allst0, writes=["mean"])
                P.add("dve", "tensor_tensor", out=stat[:, 3, :], in0=stat[:, 2, :], in1=stat[:, 2, :], op=ALU.mult,
                      reads=["mean"], writes=["msq"])
                P.add("dve", "scalar_tensor_tensor", out=stat[:, 4, :], in0=stat[:, 1, :], scalar=1.0 / 128, in1=stat[:, 3, :],
                      op0=ALU.mult, op1=ALU.subtract,
                      reads=allst1 + ["msq"], writes=["var"])
                P.add("dve", "tensor_scalar", out=stat[:, 4, :], in0=stat[:, 4, :], scalar1=EPS, scalar2=4.0, op0=ALU.add, op1=ALU.mult,
                      reads=["var"], writes=["var"])
                P.add("pool", "tensor_tensor", out=stat[:, 5, :], in0=stat[:, 4, :], in1=ctab[:, C_MH:C_MH + T], op=ALU.pow,
                      reads=["var", "ctab"], writes=["rstd"])
                yield
                for t in range(T):
                    P.tag = "ret_norm"
                    P.add("dve", "tensor_scalar", out=Ot[:, t, :], in0=Ot[:, t, :], scalar1=stat[:, 2, t:t + 1],
                          scalar2=stat[:, 5, t:t + 1], op0=ALU.subtract, op1=ALU.mult,
                          reads=[("Ot", t), "mean", "rstd"], writes=[("Ot", t)])
                    P.add("pool", "tensor_tensor", out=Mst[:, t, :], in0=Ot[:, t, :], in1=GGb[:, t, :], op=ALU.mult,
                          reads=[("Ot", t), ("GG", par, t)], writes=[mk])
                    yield
                fcol = i * 128
            else:
                corr = ctab[:, C_CORR + i * 128:C_CORR + (i + 1) * 128]
                ngrp = (T + 1) // 2
                its = []
                first_j = {}
                last_j = {}
                for g in range(ngrp):
                    qt = [t for t in (2 * g, 2 * g + 1) if t < T]
                    reg = list(range(qt[0]))
                    half = len(reg) // 2
                    order = reg[:half] + list(range(qt[0], qt[-1] + 1)) + reg[half:]
                    for j in order:
                        its.append((g, j, qt))
                        for t in qt:
                            if t >= j:
                                first_j.setdefault(t, j)
                                last_j[t] = j

                def emit_score(n):
                    g, j, qt = its[n]
                    cols = [t for t in qt if t >= j]
                    q0 = cols[0] * 128
                    nq = len(cols) * 128
                    bank = 2 + n % NSB
                    et = ET[n % NSB]
                    ek = "ET%d" % (n % NSB)
                    pk = "ps%d" % bank
                    tgq = sorted(set([cc // 4 for cc in cols]))
                    P.add("pe", "matmul",
                          ps[bank][:, 0:nq], lhsT=K0[:, j * 128:(j + 1) * 128], rhs=A0[:, q0:q0 + nq], start=True, stop=True,
                          reads=[("K0", j // 4), "K0aug", "K0", "A0"] + [("A0", x) for x in tgq], writes=[pk])
                    P.add("pe", "matmul",
                          ps[bank][:, 256:256 + nq], lhsT=K1[:, j * 128:(j + 1) * 128], rhs=A1[:, q0:q0 + nq], start=True, stop=True,
                          reads=[("K1", j // 4), "K1aug", "K1", "A1"] + [("A1", x) for x in tgq], writes=[pk])
                    dj = j - 2 * g + 16
                    bk = ctab[:, C_BK + i * 18 + dj:C_BK + i * 18 + dj + 1]
                    if nq == 256:
                        P.add("act", "activation",
                              out=et[:, :], in_=ps[bank][:, :], func=AF.Exp, bias=bk, scale=0.125,
                              reads=[pk, "ctab"], writes=[ek])
                    else:
                        P.add("act", "activation",
                              out=et[:, :].rearrange("p (c n) -> p c n", c=2)[:, :, 0:128],
                              in_=ps[bank][:, :].rearrange("p (c n) -> p c n", c=2)[:, :, 0:128],
                              func=AF.Exp, bias=bk, scale=0.125,
                              reads=[pk, "ctab"], writes=[ek])
                    if j == cols[0]:
                        etv = et[:, :].rearrange("p (c n) -> p c n", c=2)[:, :, 0:128]
                        P.add("dve", "tensor_tensor",
                              out=etv, in0=etv, in1=corr.unsqueeze(1).broadcast_to([128, 2, 128]), op=ALU.mult,
                              reads=[ek, "ctab"], writes=[ek])

                def emit_pv(n):
                    g, j, qt = its[n]
                    cols = [t for t in qt if t >= j]
                    et = ET[n % NSB]
                    ek = "ET%d" % (n % NSB)
                    for c in range(2):
                        for ci, t in enumerate(cols):
                            ab = 4 + c * 2 + qt.index(t)
                            P.add("pe", "matmul",
                                  ps[ab][:, 0:129], lhsT=et[:, c * 256 + ci * 128:c * 256 + (ci + 1) * 128], rhs=Vt[:, j, 0:129],
                                  start=(j == first_j[t]), stop=(j == last_j[t]),
                                  reads=[ek, ("Vt", par, j), vinit], writes=["ps%d" % ab])

                def emit_fin(g, qt):
                    acs = accs[0]
                    ak = "accs0"
                    for c in range(2):
                        for qi, t in enumerate(qt):
                            ab = 4 + c * 2 + qi
                            sl = c * 2 + qi
                            if c == 0:
                                P.add("act", "activation", out=acs[:, sl, 0:129], in_=ps[ab][:, 0:129], func=AF.Copy,
                                      reads=["ps%d" % ab], writes=[(ak, sl)])
                            else:
                                P.add("dve", "tensor_copy", out=acs[:, sl, 0:129], in_=ps[ab][:, 0:129],
                                      reads=["ps%d" % ab], writes=[(ak, sl)])
                    for qi, t in enumerate(qt):
                        s0 = qi
                        s1 = 2 + qi
                        o0t = o0[t % 2]
                        ok = "o0_0"
                        P.add("dve", "reciprocal", out=stat[:, 6, t:t + 1], in_=acs[:, s0, 128:129],
                              reads=[(ak, s0)], writes=[("rd0", t)])
                        P.add("dve", "reciprocal", out=stat[:, 7, t:t + 1], in_=acs[:, s1, 128:129],
                              reads=[(ak, s1)], writes=[("rd1", t)])
                        P.add("dve", "tensor_tensor", out=stat[:, 7, t:t + 1], in0=stat[:, 7, t:t + 1], in1=lamt[:, 4:5], op=ALU.mult,
                              reads=[("rd1", t), "nlam"], writes=[("rd1", t)])
                        P.add("act", "activation", out=o0t[:, :], in_=acs[:, s0, 0:128], func=AF.Copy, scale=stat[:, 6, t:t + 1],
                              reads=[(ak, s0), ("rd0", t)], writes=[ok])
                        P.add("dve", "scalar_tensor_tensor",
                              out=Ot[:, t, :], in0=acs[:, s1, 0:128], scalar=stat[:, 7, t:t + 1], in1=o0t[:, :], op0=ALU.mult, op1=ALU.add,
                              reads=[(ak, s1), ("rd1", t), ok], writes=[("Ot", t)])
                        P.add("act", "activation", out=junk[:, 0:128], in_=Ot[:, t, :], func=AF.Square,
                              accum_out=stat[:, 1, t:t + 1],
                              reads=[("Ot", t)], writes=["junk", ("st1", t)])

                LOOK = NSB - 1
                P.tag = "diff_attn"
                for n in range(min(LOOK, len(its))):
                    emit_score(n)
                for n in range(len(its)):
                    P.tag = "diff_attn"
                    if n + LOOK < len(its):
                        emit_score(n + LOOK)
                    emit_pv(n)
                    g, j, qt = its[n]
                    if n + 1 == len(its) or its[n + 1][0] != g:
                        emit_fin(g, qt)
                    yield
                P.tag = "diff_norm"
                allst1 = [("st1", t) for t in range(T)]
                P.add("dve", "tensor_scalar", out=stat[:, 4, :], in0=stat[:, 1, :], scalar1=1.0 / 128, scalar2=EPS, op0=ALU.mult, op1=ALU.add,
                      reads=allst1, writes=["var"])
                P.add("pool", "tensor_tensor", out=stat[:, 5, :], in0=stat[:, 4, :], in1=ctab[:, C_MH:C_MH + T], op=ALU.pow,
                      reads=["var", "ctab"], writes=["rstd"])
                yield
                for t in range(T):
                    P.tag = "diff_norm"
                    P.add("dve", "scalar_tensor_tensor",
                          out=Mst[:, t, :], in0=Ot[:, t, :], scalar=stat[:, 5, t:t + 1], in1=GGb[:, t, :], op0=ALU.mult, op1=ALU.mult,
                          reads=[("Ot", t), "rstd", ("GG", par, t)], writes=[mk])
                    yield
                fcol = ML // 2 + i * 128

            ckpt("head%d" % hh)
            P.add("sp", "dma_start",
                  out=m_h[0 if hh < SPLIT else 1][:, (hh if hh < SPLIT else hh - SPLIT) * 128:(hh if hh < SPLIT else hh - SPLIT) * 128 + 128].rearrange("(t p) c -> p t c", p=128), in_=Mst[:, :, :],
                  reads=[mk], writes=[("md", hh)], dma=True)
            yield

        def units_proj(hh):
            return 2 * (2 * NTG + T) + (T if hh % 2 == 1 else 1)

        def units_mix(hh):
            if hh % 2 == 1:
                return T + 1 + T + 1
            n_its = sum(min(2 * g + 1, T - 1) + 1 for g in range((T + 1) // 2))
            return n_its + 1 + T + 1

        def exchange(half):
            rk = [("md", hh) for hh in (range(SPLIT) if half == 0 else range(SPLIT, NHL))]
            for k, (c0, c1) in enumerate(XCH[half]):
                P.add("pool", "collective_compute", "AllGather", ALU.bypass, replica_groups=RGROUPS,
                      ins=[m_h[half][c0:c1, :]], outs=[mf_d[half][k][:, :]], reads=rk, writes=[("mfull", half, k)], dma=True)

        for _ in proj_gen(0):
            pass
        for hh in range(NH):
            gm = mixer_gen(hh)
            if hh + 1 < NH:
                gp = proj_gen(hh + 1)
                nm = units_mix(hh)
                npj = units_proj(hh + 1)
                dm = dp = 0
                gm_done = gp_done = False
                while not (gm_done and gp_done):
                    if hh == SPLIT and dm == 8:
                        exchange(0)
                        dm += 1
                    if not gm_done:
                        try:
                            next(gm)
                            dm += 1
                        except StopIteration:
                            gm_done = True
                    while not gp_done and (gm_done or dp * nm < (dm + 1) * npj):
                        try:
                            next(gp)
                            dp += 1
                        except StopIteration:
                            gp_done = True
            else:
                for _ in gm:
                    pass

        ckpt("heads")
        P.barrier()
        P.tag = "p3_tr"
        exchange(1)
        def p3_tr(t):
            P.tag = "p3_tr"
            ubuf = ub[t % 2]
            uk = "ub%d" % (t % 2)
            for half in range(2):
                tpc = XROWS[half] // 128
                P.add("sp", "dma_start", out=ubuf[:, XB[half]:XB[half] + 2 * XW[half]].rearrange("p (r c) -> p r c", r=2),
                      in_=mf_d[half][t // tpc].rearrange("(r l) c -> l r c", r=2)[(t % tpc) * 128:(t % tpc + 1) * 128, :, :],
                      reads=[("mfull", half, t // tpc)], writes=[uk if half == 0 else (uk, "B")], dma=True)
            for q4 in range(KC // 4):
                bank = 4 + psrot["a"] % 4
                psrot["a"] += 1
                pk = "ps%d" % bank
                pbf = ps[bank][:, :].bitcast(BF16)
                for ii in range(4):
                    kc = q4 * 4 + ii
                    P.add("pe", "transpose",
                        out=pbf[:, ii * 128:(ii + 1) * 128], in_=ubuf[:, kc * 128:(kc + 1) * 128], identity=ident,
                        reads=[uk, "btab"] + ([(uk, "B")] if (kc + 1) * 128 > XB[1] else []), writes=[pk])
                P.add("dve", "tensor_copy",
                    out=actT[:, q4 * 4:q4 * 4 + 4, t * 128:(t + 1) * 128],
                    in_=pbf[:, 0:512].rearrange("p (a b) -> p a b", b=128),
                    reads=[pk], writes=[("aT", q4 * 4 + ii, t) for ii in range(4)])
        last_layer = (lidx == NL - 1)
        P.tag = "p3_mm"
        p3 = [(jn, t) for jn in range(NO) for t in range(T)]
        NSL = 2 * NO

        def p3_load(n):
            jn, t = p3[n]
            slot = n % NSL
            hqb = hb[slot // NO][:, (slot % NO) * 512:(slot % NO + 1) * 512]
            src, skeys = hsrc(lidx == 0, t, jn * 512, (jn + 1) * 512)
            P.add("sp", "dma_start", out=hqb, in_=src,
                  reads=skeys + [("hq", t, jn), "hb%d" % (slot // NO)], writes=[("hbq", slot)], dma=True)

        PF = min(3, NSL - 1)
        for n in range(min(PF, len(p3))):
            p3_load(n)
        wb = None
        p3_tr(0)
        for n, (jn, t) in enumerate(p3):
            if jn == 0 and t + 1 < T:
                p3_tr(t + 1)
            P.tag = "p3_mm"
            if t == 0:
                gu = li * NU + NHL + jn
                b = wunit_buf[gu]
                wb = Wb[b]
                wk = "W%d" % b
                prefetch(1)
            if n + PF < len(p3):
                p3_load(n + PF)
            bank = psrot["b"] % 4
            psrot["b"] += 1
            pk = "ps%d" % bank
            slot = n % NSL
            hqb = hb[slot // NO][:, (slot % NO) * 512:(slot % NO + 1) * 512]
            hqk = ("hbq", slot)
            hbk = "hb%d" % (slot // NO)
            for kc in range(KC):
                P.add("pe", "matmul",
                      ps[bank][:, :], lhsT=actT[:, kc, t * 128:(t + 1) * 128], rhs=wb[:, kc, :],
                      start=(kc == 0), stop=(kc == KC - 1),
                      reads=[wk, ("aT", kc, t)], writes=[pk])
            P.add("dve", "tensor_tensor", out=hqb, in0=ps[bank][:, :], in1=hqb, op=ALU.add,
                  reads=[pk, hqk, hbk], writes=[hqk])
            P.add("act", "dma_start", out=h_d[t * 128:(t + 1) * 128, jn * 512:(jn + 1) * 512], in_=hqb,
                  reads=[hqk, hbk], writes=[("hq", t, jn), ("hw", t, jn)], dma=True)
        for t in range(T):
            P.add("sp", "nop", reads=[("hw", t, jn) for jn in range(NO)], writes=hkeys(t))

    ckpt("phase3")
    P.tag = "final"
    if final:
        P.barrier()
        P.add("sp", "dma_start", out=Gbuf[:, :], in_=gtab_d[DEPTH], writes=["Gbuf"], dma=True)
        flat = actT[:, :, :].rearrange("p k l -> p (k l)")
        NFB = 4
        fbs = [flat[:, i * 2 * D:(i + 1) * 2 * D].bitcast(F32) for i in range(NFB)]

        def fin_store(t):
            fb = fbs[t % NFB]
            fk = "fb%d" % (t % NFB)
            p0 = max(t * 128, 64)
            p1 = min((t + 1) * 128, 64 + SEQ)
            P.add("act", "dma_start",
                  out=y_d[p0 - 64:p1 - 64, :], in_=fb[p0 - t * 128:p1 - t * 128, :],
                  reads=[fk], writes=[("y", t)], dma=True)

        for t in range(T):
            fb = fbs[t % NFB]
            fk = "fb%d" % (t % NFB)
            P.add("sp", "dma_start", out=fb[:, :], in_=h_d[t * 128:(t + 1) * 128, :],
                  reads=hkeys(t), writes=[fk], dma=True)
            P.add("act", "activation", out=ub[t % 2][:, :], in_=fb[:, :], func=AF.Square, accum_out=sm[:, t:t + 1],
                  reads=[fk], writes=["ub%d" % (t % 2), ("sm", t)])
            P.add("act", "activation", out=sm[:, 32 + t:33 + t], in_=sm[:, t:t + 1], func=AF.Sqrt,
                  bias=ctab[:, C_EPS:C_EPS + 1], scale=1.0 / D,
                  reads=[("sm", t), "ctab"], writes=[("sm2", t)])
            P.add("dve", "reciprocal", out=sm[:, 32 + t:33 + t], in_=sm[:, 32 + t:33 + t],
                  reads=[("sm2", t)], writes=[("sm2", t)])
            P.add("dve", "scalar_tensor_tensor",
                  out=fb[:, :], in0=fb[:, :], scalar=sm[:, 32 + t:33 + t], in1=Gbuf[:, :], op0=ALU.mult, op1=ALU.mult,
                  reads=[fk, ("sm2", t), "Gbuf"], writes=[fk])
            if t >= 1:
                fin_store(t - 1)
        fin_store(T - 1)
    else:
        for t in range(T):
            hbuf = hb[t % 2]
            hk = "hb%d" % (t % 2)
            P.add("sp", "dma_start", out=hbuf[:, :], in_=h_d[t * 128:(t + 1) * 128, :],
                  reads=hkeys(t), writes=[hk], dma=True)
            P.add("sp", "dma_start", out=ho_d[t * 128:(t + 1) * 128, :], in_=hbuf[:, :],
                  reads=[hk], writes=[("ho", t)], dma=True)

    P.emit(nc)
    st.close()
    return nc, P


def _const_tables():
    ct = np.zeros((128, C_END), np.float32)
    p = np.arange(128)
    logg = np.log(1.0 - 2.0 ** (-5.0 - np.arange(H, dtype=np.float64)))
    slopes = 2.0 ** (-8.0 * np.arange(1, H + 1, dtype=np.float64) / H)
    tt = p % 64
    same = (p[:, None] // 64) == (p[None, :] // 64)
    for i in range(H):
        e = np.abs(tt[None, :] - tt[:, None]) - (tt[None, :] + 1.0)
        ct[:, C_DM + i * 128:C_DM + (i + 1) * 128] = np.where(same, np.exp(logg[i] * e), 0.0)
        ct[:, C_XI + i * 64:C_XI + (i + 1) * 64] = np.exp(logg[i] * (np.arange(64) + 1.0))[None, :]
        ct[:, C_ZETA + i] = np.exp(logg[i] * (63.0 - tt))
        ct[:, C_DEC + i] = np.exp(logg[i] * 64.0)
        for dj in range(18):
            ct[:, C_BK + i * 18 + dj] = slopes[i] * (p + 128.0 * (dj - 16))
        k = p[:, None]
        q = p[None, :]
        cr = np.where(k > q, np.exp(-2.0 * slopes[i] * (k - q)), 1.0)
        cr = np.where((k >= 64) & (q < 64), 0.0, cr)
        ct[:, C_CORR + i * 128:C_CORR + (i + 1) * 128] = cr
        ct[:, C_SL8 + i] = 8.0 * slopes[i]
    ct[:, C_EPS] = EPS
    ct[:, C_MH:C_MH + T] = -0.5
    bt = np.zeros((128, B_END), np.float32)
    bt[:, B_ID:B_ID + 128] = np.eye(128)
    pos = np.arange(T)[None, :] * 128 + p[:, None]
    bt[:, B_VAL:B_VAL + T] = ((pos >= 48) & (pos < 64 + SEQ)).astype(np.float32)
    qpos = np.arange(L)
    qa = np.stack([-(qpos % 128), -128.0 * ((qpos // 128) % 2)]).astype(np.float32)
    return ct, bt.astype(ml_dtypes.bfloat16), qa.astype(ml_dtypes.bfloat16)


def _prep_shared(norm_g, w_in, w_out, ret_norm_g, diff_norm_g, lq1, lk1, lq2, lk2, final_norm_g):
    ct, bt, qa = _const_tables()
    lqt = np.empty((DEPTH, 128, 256), np.float32)
    for li in range(DEPTH):
        ct[:, C_NG + li * KC:C_NG + (li + 1) * KC] = norm_g[li].reshape(KC, 128).T
        lqt[li] = np.concatenate([lq1[li], lk1[li], lq2[li], lk2[li]])[None, :]
    slots = ((C_DM, 128), (C_XI, 64), (C_ZETA, 1), (C_DEC, 1), (C_BK, 18), (C_CORR, 128), (C_SL8, 1))
    perm = np.empty(D, np.int64)
    for r in range(2):
        for hh in range(NHL):
            fam, il = 1 - hh % 2, hh // 2
            half, q = (0, hh) if hh < SPLIT else (1, hh - SPLIT)
            f0 = XB[half] + r * XW[half] + q * 128
            g0 = fam * (D // 2) + (2 * il + r) * 128
            perm[f0:f0 + 128] = np.arange(g0, g0 + 128)
    per_r = []
    for r in range(2):
        ctr = ct.copy()
        for il in range(HL):
            gi = 2 * il + r
            for (c0, w) in slots:
                ctr[:, c0 + il * w:c0 + (il + 1) * w] = ct[:, c0 + gi * w:c0 + (gi + 1) * w]
        gt = np.zeros((DEPTH + 1, 128, D), np.float32)
        for li in range(DEPTH):
            for il in range(HL):
                gi = 2 * il + r
                gt[li, :, il * 128:(il + 1) * 128] = ret_norm_g[li][None, gi * 128:(gi + 1) * 128]
                gt[li, :, D // 2 + il * 128:D // 2 + (il + 1) * 128] = diff_norm_g[li][None, gi * 128:(gi + 1) * 128]
        gt[DEPTH] = final_norm_g[None, :]
        wst = np.empty((DEPTH * NU, 128, KC, 512), np.float32)
        for li in range(DEPTH):
            w = w_in[li].reshape(KC, 128, 8, H, 128)
            for hh in range(NHL):
                gi = 2 * (hh // 2) + r
                fam0 = 0 if hh % 2 == 1 else 4
                blk = w[:, :, fam0:fam0 + 4, gi, :]
                wst[li * NU + hh] = blk.transpose(1, 0, 2, 3).reshape(128, KC, 512)
            wo = w_out[li][perm].reshape(KC, 128, NO, 512)
            for j in range(NO):
                wst[li * NU + NHL + j] = wo[:, :, j, :].transpose(1, 0, 2)
        per_r.append((ctr, gt, wst.reshape(DEPTH * NU, 128, KC * 512)))
    return per_r, bt, qa, lqt


_CACHE = {}
_DEBUG = None


def kernel(x, meta_tokens, norm_g, w_in, w_out, ret_norm_g, diff_norm_g,
           lambda_q1, lambda_k1, lambda_q2, lambda_k2, final_norm_g):
    f = lambda a: np.ascontiguousarray(np.asarray(a, dtype=np.float32))
    x = f(x)
    per_r, bt, qa, lqt = _prep_shared(f(norm_g), f(w_in), f(w_out), f(ret_norm_g), f(diff_norm_g),
                                      f(lambda_q1), f(lambda_k1), f(lambda_q2), f(lambda_k2), f(final_norm_g))
    meta = f(meta_tokens)
    if "nc" not in _CACHE:
        _CACHE["nc"] = build_program(debug=_DEBUG)[0]
    nc = _CACHE["nc"]
    in_maps = []
    for c in range(NCORES):
        ctr, gt, wst = per_r[c % 2]
        in_maps.append({"x": x[c // 2], "meta": meta, "wst": wst, "ctab": ctr, "btab": bt, "qa": qa, "gtab": gt, "lqt": lqt})
    res = run_bass_kernel_spmd(nc, in_maps, core_ids=list(range(NCORES)))
    out = np.stack([np.asarray(res.results[2 * b]["y"], dtype=np.float32) for b in range(NCORES // 2)], axis=0)
    return out
```
